# Optimizing a Trainium2 kernel written in Bass

```python
import jax, jax.numpy as jnp
from jax import lax
import numpy as np

D_MODEL = 2048
BATCH = 2
SEQ = 16384
DEPTH = 4

GRID_W = 64
CTX_LEN = 256
N_MIXERS = 3
D_BRANCH = D_MODEL
NORM_EPS = 1e-6
FOURIER_GROUPS = 8
HEAD_DIM = 64
N_HEADS = D_BRANCH // HEAD_DIM
N_KV_HEADS = 4
GQA_GROUP = N_HEADS // N_KV_HEADS
WINDOW = 128
BLOCK = 128
BAND = 3 * BLOCK
ROPE_BASE = 10000.0
ATTN_SCALE = HEAD_DIM ** -0.5
NEG_INF = -1e30
ATTN_IN = (N_HEADS + 2 * N_KV_HEADS) * HEAD_DIM + D_BRANCH
CONV_WIDTH = 31
N_FOURIER_LAYERS = (DEPTH + 2) // 3
N_ATTN_LAYERS = (DEPTH + 1) // 3
N_CONV_LAYERS = DEPTH // 3

kernel_name = 'hybrid_fourier_window_conformer_dit'


def rms_norm(x, g):
    xf = x.astype(jnp.float32)
    y = xf * lax.rsqrt(jnp.mean(xf * xf, axis=-1, keepdims=True) + NORM_EPS)
    return (y * g.astype(jnp.float32)).astype(x.dtype)


def layer_norm(x, g, b):
    xf = x.astype(jnp.float32)
    mu = jnp.mean(xf, axis=-1, keepdims=True)
    var = jnp.mean(jnp.square(xf - mu), axis=-1, keepdims=True)
    y = (xf - mu) * lax.rsqrt(var + NORM_EPS) * g.astype(jnp.float32) + b.astype(jnp.float32)
    return y.astype(x.dtype)


def ada_modulation(cond, w, b):
    m = jax.nn.silu(cond) @ w + b
    return jnp.split(m, 3, axis=-1)


def axial_rope_tables(n_tokens):
    rows = n_tokens // GRID_W
    row = jnp.repeat(jnp.arange(rows), GRID_W).astype(jnp.float32)
    col = jnp.tile(jnp.arange(GRID_W), rows).astype(jnp.float32)
    quarter = HEAD_DIM // 4
    inv_freq = ROPE_BASE ** (-jnp.arange(quarter, dtype=jnp.float32) / quarter)
    ang_r = row[:, None] * inv_freq[None, :]
    ang_c = col[:, None] * inv_freq[None, :]
    ang = jnp.concatenate([ang_r, ang_r, ang_c, ang_c], axis=-1)
    return jnp.cos(ang), jnp.sin(ang)


def apply_rope(x, cos, sin):
    n = x.shape[1]
    bshape = (1, n) + (1,) * (x.ndim - 3) + (HEAD_DIM,)
    xf = x.astype(jnp.float32)
    xr = xf.reshape(x.shape[:-1] + (2, 2, HEAD_DIM // 4))
    rot = jnp.stack([-xr[..., 1, :], xr[..., 0, :]], axis=-2).reshape(x.shape)
    return (xf * cos.reshape(bshape) + rot * sin.reshape(bshape)).astype(x.dtype)


def fourier_mix(h, w_in, w_out):
    b, n, _ = h.shape
    u, z = jnp.split(h @ w_in, 2, axis=-1)
    ug = u.reshape(b, n, FOURIER_GROUPS, D_BRANCH // FOURIER_GROUPS).astype(jnp.float32)
    mixed = jnp.fft.fft2(ug, axes=(1, 3), norm='ortho').real
    y = mixed.reshape(b, n, D_BRANCH).astype(h.dtype) * jax.nn.silu(z)
    return y @ w_out


def conformer_conv_mix(h, w_in, dw_w, dw_b, ln_g, ln_b, w_out):
    a, a_gate, z = jnp.split(h @ w_in, 3, axis=-1)
    g = a * jax.nn.sigmoid(a_gate)
    pad = CONV_WIDTH // 2
    y = lax.conv_general_dilated(
        g, dw_w[:, None, :].astype(g.dtype), window_strides=(1,),
        padding=[(pad, pad)], dimension_numbers=('NWC', 'WIO', 'NWC'),
        feature_group_count=D_BRANCH) + dw_b
    y = jax.nn.silu(layer_norm(y, ln_g, ln_b))
    return (y * jax.nn.silu(z)) @ w_out


def split_qkvz(h, w_in):
    b, n, _ = h.shape
    proj = h @ w_in
    q_end = N_HEADS * HEAD_DIM
    k_end = q_end + N_KV_HEADS * HEAD_DIM
    v_end = k_end + N_KV_HEADS * HEAD_DIM
    q = proj[..., :q_end].reshape(b, n, N_KV_HEADS, GQA_GROUP, HEAD_DIM)
    k = proj[..., q_end:k_end].reshape(b, n, N_KV_HEADS, HEAD_DIM)
    v = proj[..., k_end:v_end].reshape(b, n, N_KV_HEADS, HEAD_DIM)
    z = proj[..., v_end:]
    return q, k, v, z


def sink_softmax(s, sink):
    s_sink = jnp.broadcast_to(sink[None, :, :, None, None], s.shape[:-1] + (1,))
    p = jax.nn.softmax(jnp.concatenate([s, s_sink], axis=-1), axis=-1)
    return p[..., :-1]


def window_attention_mix(h, hc, w_in, sink, w_out, cos, sin, with_ctx_out):
    b, n, _ = h.shape
    n_ctx = hc.shape[1]
    n_blocks = n // BLOCK
    q, k, v, z = split_qkvz(h, w_in)
    qc, kc, vc, zc = split_qkvz(hc, w_in)
    q = apply_rope(q, cos, sin) * ATTN_SCALE
    k = apply_rope(k, cos, sin)
    sink = sink.reshape(N_KV_HEADS, GQA_GROUP).astype(jnp.float32)
    pad = ((0, 0), (BLOCK, BLOCK), (0, 0), (0, 0))
    kp = jnp.pad(k, pad)
    vp = jnp.pad(v, pad)
    q_off = jnp.arange(BLOCK)[:, None]
    k_off = jnp.arange(BAND)[None, :] - BLOCK
    rel_ok = jnp.abs(q_off - k_off) <= WINDOW

    def block(bi):
        start = bi * BLOCK
        qb = lax.dynamic_slice_in_dim(q, start, BLOCK, axis=1)
        kb = lax.dynamic_slice_in_dim(kp, start, BAND, axis=1)
        vb = lax.dynamic_slice_in_dim(vp, start, BAND, axis=1)
        kpos = start + k_off
        valid = rel_ok & (kpos >= 0) & (kpos < n)
        s_loc = jnp.einsum('bqhgd,bkhd->bhgqk', qb, kb).astype(jnp.float32)
        s_loc = jnp.where(valid, s_loc, NEG_INF)
        s_ctx = jnp.einsum('bqhgd,bkhd->bhgqk', qb, kc).astype(jnp.float32)
        p = sink_softmax(jnp.concatenate([s_loc, s_ctx], axis=-1), sink).astype(v.dtype)
        return (jnp.einsum('bhgqk,bkhd->bqhgd', p[..., :BAND], vb)
                + jnp.einsum('bhgqk,bkhd->bqhgd', p[..., BAND:], vc))

    o = lax.map(block, jnp.arange(n_blocks))
    o = jnp.moveaxis(o, 0, 1).reshape(b, n, D_BRANCH)
    out = (o * jax.nn.silu(z)) @ w_out
    if not with_ctx_out:
        return out, None
    s_c = jnp.einsum('bqhgd,bkhd->bhgqk', qc * ATTN_SCALE, kc).astype(jnp.float32)
    p_c = sink_softmax(s_c, sink).astype(vc.dtype)
    oc = jnp.einsum('bhgqk,bkhd->bqhgd', p_c, vc).reshape(b, n_ctx, D_BRANCH)
    out_c = (oc * jax.nn.silu(zc)) @ w_out
    return out, out_c


def setup_inputs(seed: int = 0) -> dict:
    key = jax.random.key(seed)
    ks = jax.random.split(key, 20)
    f32 = jnp.float32

    def nrm(k, shape, s):
        return jax.random.normal(k, shape, f32) * s

    d, e = D_MODEL, D_BRANCH
    return {
        'x': nrm(ks[0], (BATCH, SEQ, d), 1.0),
        'c': nrm(ks[1], (BATCH, d), 1.0),
        'ctx': nrm(ks[2], (BATCH, CTX_LEN, d), 1.0),
        'c_ctx': nrm(ks[3], (d,), 1.0),
        'norm_g': 1.0 + nrm(ks[4], (DEPTH, d), 0.05),
        'ada_w': nrm(ks[5], (DEPTH, d, 3 * d), 0.5 * d ** -0.5),
        'ada_b': nrm(ks[6], (DEPTH, 3 * d), 0.02),
        'four_w_in': nrm(ks[7], (N_FOURIER_LAYERS, d, 2 * e), d ** -0.5),
        'four_w_out': nrm(ks[8], (N_FOURIER_LAYERS, e, d), e ** -0.5),
        'attn_w_in': nrm(ks[9], (N_ATTN_LAYERS, d, ATTN_IN), d ** -0.5),
        'attn_sink': nrm(ks[10], (N_ATTN_LAYERS, N_HEADS), 0.5),
        'attn_w_out': nrm(ks[11], (N_ATTN_LAYERS, e, d), e ** -0.5),
        'conv_w_in': nrm(ks[12], (N_CONV_LAYERS, d, 3 * e), d ** -0.5),
        'conv_dw_w': nrm(ks[13], (N_CONV_LAYERS, CONV_WIDTH, e), CONV_WIDTH ** -0.5),
        'conv_dw_b': nrm(ks[14], (N_CONV_LAYERS, e), 0.02),
        'conv_ln_g': 1.0 + nrm(ks[15], (N_CONV_LAYERS, e), 0.05),
        'conv_ln_b': nrm(ks[16], (N_CONV_LAYERS, e), 0.02),
        'conv_w_out': nrm(ks[17], (N_CONV_LAYERS, e, d), e ** -0.5),
        'final_g': 1.0 + nrm(ks[18], (d,), 0.05),
    }


def reference(x, c, ctx, c_ctx, norm_g, ada_w, ada_b, four_w_in, four_w_out,
              attn_w_in, attn_sink, attn_w_out, conv_w_in, conv_dw_w, conv_dw_b,
              conv_ln_g, conv_ln_b, conv_w_out, final_g):
    n_tokens = x.shape[1]
    cos, sin = axial_rope_tables(n_tokens)
    for i in range(DEPTH):
        kind, j = i % N_MIXERS, i // N_MIXERS
        with_ctx = i < DEPTH - 1
        shift, scale, gate = ada_modulation(c, ada_w[i], ada_b[i])
        h = rms_norm(x, norm_g[i]) * (1.0 + scale[:, None, :]) + shift[:, None, :]
        shift_c, scale_c, gate_c = ada_modulation(c_ctx, ada_w[i], ada_b[i])
        hc = rms_norm(ctx, norm_g[i]) * (1.0 + scale_c) + shift_c
        if kind == 0:
            o = fourier_mix(h, four_w_in[j], four_w_out[j])
            oc = fourier_mix(hc, four_w_in[j], four_w_out[j]) if with_ctx else None
        elif kind == 1:
            o, oc = window_attention_mix(h, hc, attn_w_in[j], attn_sink[j], attn_w_out[j],
                                         cos, sin, with_ctx)
        else:
            conv_args = (conv_w_in[j], conv_dw_w[j], conv_dw_b[j], conv_ln_g[j],
                         conv_ln_b[j], conv_w_out[j])
            o = conformer_conv_mix(h, *conv_args)
            oc = conformer_conv_mix(hc, *conv_args) if with_ctx else None
        x = x + gate[:, None, :] * o
        if with_ctx:
            ctx = ctx + gate_c * oc
    return rms_norm(x, final_g)
```

```python
import numpy as np
import concourse.bass as bass
import concourse.mybir as mybir
from concourse.bass_utils import run_bass_kernel_spmd

F32 = mybir.dt.float32
BF16 = mybir.dt.bfloat16
AF = mybir.ActivationFunctionType
ALU = mybir.AluOpType
AX = mybir.AxisListType

NCORES = 8
D = 2048
KC = D // 128


class Tok:
    __slots__ = ("name", "writers", "readers", "sem", "cnt")

    def __init__(self, name):
        self.name = name
        self.writers = []
        self.readers = []
        self.sem = None
        self.cnt = 0


class Op:
    __slots__ = ("eng", "fn", "cdeps", "dwaits", "is_dma", "tok", "val", "signal")

    def __init__(self, eng, fn):
        self.eng = eng
        self.fn = fn
        self.cdeps = []
        self.dwaits = []
        self.is_dma = False
        self.tok = None
        self.val = 0
        self.signal = False


class Prog:
    ENGS = ["sync", "scalar", "vector", "gpsimd", "tensor"]

    def __init__(self, nc):
        self.nc = nc
        self.q = {e: [] for e in self.ENGS}
        self.dma_toks = []
        self.nsem = 0

    def tok(self, name="t"):
        return Tok(name)

    def toks(self, name, n):
        return [Tok(f"{name}{i}") for i in range(n)]

    def _dep_on(self, op, prev):
        if prev.is_dma:
            op.dwaits.append((prev.tok, prev.tok.cnt))
        else:
            op.cdeps.append(prev)

    def _track(self, op, reads, writes, pwrites):
        for t in reads:
            for w in t.writers:
                self._dep_on(op, w)
        for t in writes:
            for w in t.writers:
                self._dep_on(op, w)
            for r in t.readers:
                self._dep_on(op, r)
        for t in pwrites:
            for r in t.readers:
                self._dep_on(op, r)
        for t in reads:
            t.readers.append(op)
        for t in writes:
            t.writers = [op]
            t.readers = []
        for t in pwrites:
            if t.readers:
                t.writers = [op]
                t.readers = []
            else:
                t.writers.append(op)

    def op(self, eng, fn, reads=(), writes=(), pwrites=()):
        o = Op(eng, fn)
        self._track(o, reads, writes, pwrites)
        self.q[eng].append(o)
        return o

    def dma(self, eng, out, in_, reads=(), writes=(), pwrites=(), semtok=None, **kw):
        if semtok is None:
            semtok = (list(writes) + list(pwrites) + list(reads))[0]
        o = Op(eng, lambda e: e.dma_start(out=out, in_=in_, **kw))
        o.is_dma = True
        o.tok = semtok
        self._track(o, reads, writes, pwrites)
        if semtok.sem is None:
            semtok.sem = self.nc.alloc_semaphore(f"d{self.nsem}_{semtok.name}")
            self.nsem += 1
            self.dma_toks.append(semtok)
        semtok.cnt += 16
        self.q[eng].append(o)
        return o

    def emit(self):
        nc = self.nc
        comp = ["scalar", "vector", "gpsimd", "tensor"]
        for e in self.ENGS:
            for o in self.q[e]:
                for d in o.cdeps:
                    d.signal = True
        esem = {}
        for e in comp:
            n = 0
            for o in self.q[e]:
                if o.signal and not o.is_dma:
                    n += 1
                    o.val = n
            if n:
                esem[e] = nc.alloc_semaphore(f"e_{e}")
                self.nsem += 1
        final_waits = [(t.sem, t.cnt) for t in self.dma_toks]
        q = self.q

        def run(e, eng):
            waited = {}
            for o in q[e]:
                ws = [(esem[d.eng], d.val) for d in o.cdeps] + [(t.sem, c) for t, c in o.dwaits]
                for s, v in ws:
                    if waited.get(id(s), 0) < v:
                        eng.wait_ge(s, v)
                        waited[id(s)] = v
                ins = o.fn(eng)
                if o.is_dma:
                    ins.then_inc(o.tok.sem, 16)
                elif o.signal:
                    ins.then_inc(esem[e], 1)
            if e == "sync":
                for s, v in final_waits:
                    if waited.get(id(s), 0) < v:
                        eng.wait_ge(s, v)

        with nc.Block() as block:
            @block.sync
            def _(eng):
                run("sync", eng)

            @block.scalar
            def _(eng):
                run("scalar", eng)

            @block.vector
            def _(eng):
                run("vector", eng)

            @block.gpsimd
            def _(eng):
                run("gpsimd", eng)

            @block.tensor
            def _(eng):
                run("tensor", eng)


def build_mod():
    nc = bass.Bass("TRN2", target_bir_lowering=False)
    cT = nc.dram_tensor("cT", [128, KC * 3], F32, kind="ExternalInput").ap()
    w = nc.dram_tensor("w", [4, D, 768], F32, kind="ExternalInput").ap()
    b = nc.dram_tensor("b", [4, 768], F32, kind="ExternalInput").ap()
    out = nc.dram_tensor("out", [4, 3, 768], F32, kind="ExternalOutput").ap()
    P = Prog(nc)
    c_sb = nc.alloc_sbuf_tensor("c_sb", [128, KC * 3], F32)
    sc_sb = nc.alloc_sbuf_tensor("sc_sb", [128, KC * 3], F32)
    w_sb = [nc.alloc_sbuf_tensor(f"w_sb{i}", [128, KC, 768], F32) for i in range(2)]
    b_sb = nc.alloc_sbuf_tensor("b_sb", [3, 4 * 768], F32)
    o_sb = nc.alloc_sbuf_tensor("o_sb", [3, 4 * 768], F32)
    ps = [nc.alloc_psum_tensor(f"ps{i}", [128, 512], F32) for i in range(2)]
    t_c, t_sc, t_b, t_o = P.tok("c"), P.tok("sc"), P.tok("b"), P.tok("o")
    t_w = P.toks("w", 2)
    t_ps = P.toks("ps", 2)

    P.dma("sync", c_sb[:, :], cT, writes=[t_c])
    for l in range(4):
        P.dma("sync", b_sb[:, l * 768:(l + 1) * 768], b[l:l + 1, :].broadcast_to([3, 768]), pwrites=[t_b])
    P.op("scalar", lambda e: e.activation(out=sc_sb[:, :], in_=c_sb[:, :], func=AF.Silu),
         reads=[t_c], writes=[t_sc])
    n = 0
    for l in range(4):
        wb = l % 2
        P.dma("sync", w_sb[wb][:, :, :], w[l].rearrange("(kc p) n -> p kc n", p=128), writes=[t_w[wb]])
        for h in range(2):
            pb = n % 2
            n += 1

            def mm(e, wb=wb, h=h, pb=pb):
                ins = None
                for kc in range(KC):
                    ins = e.matmul(ps[pb][0:3, 0:384], lhsT=sc_sb[:, kc * 3:(kc + 1) * 3],
                                   rhs=w_sb[wb][:, kc, h * 384:(h + 1) * 384],
                                   start=(kc == 0), stop=(kc == KC - 1))
                return ins
            P.op("tensor", mm, reads=[t_sc, t_w[wb]], writes=[t_ps[pb]])
            sl = slice(l * 768 + h * 384, l * 768 + (h + 1) * 384)
            P.op("vector", lambda e, pb=pb, sl=sl: e.tensor_tensor(
                out=o_sb[:, sl], in0=ps[pb][0:3, 0:384], in1=b_sb[:, sl], op=ALU.add),
                reads=[t_ps[pb], t_b], pwrites=[t_o])
    P.dma("sync", out.rearrange("l r n -> r l n"), o_sb[:, :].rearrange("r (l n) -> r l n", l=4),
          reads=[t_o], writes=[P.tok("out")])
    P.emit()
    return nc


def run_mod(c, c_ctx, ada_w, ada_b):
    cc = np.concatenate([c, c_ctx[None, :]], axis=0)
    cT = np.ascontiguousarray(cc.reshape(3, KC, 128).transpose(2, 1, 0)).reshape(128, KC * 3)
    nc = build_mod()
    in_maps = []
    for k in range(NCORES):
        in_maps.append({
            "cT": cT,
            "w": np.ascontiguousarray(ada_w[:, :, 768 * k:768 * (k + 1)]),
            "b": np.ascontiguousarray(ada_b[:, 768 * k:768 * (k + 1)]),
        })
    res = run_bass_kernel_spmd(nc, in_maps, core_ids=list(range(NCORES)))
    return np.concatenate([r["out"] for r in res.results], axis=2)


BF = None


def _bf16():
    import ml_dtypes
    return ml_dtypes.bfloat16


def const_tables():
    bf = _bf16()
    t = {}
    t["ident"] = np.eye(128, dtype=np.float32).astype(bf)
    c = np.arange(256)[:, None].astype(np.float64)
    cp = np.arange(256)[None, :].astype(np.float64)
    ang = 2 * np.pi * c * cp / 256.0
    C, S = np.cos(ang) / 16.0, np.sin(ang) / 16.0
    fc = np.zeros((256, 2, 2, 128))
    for half in range(2):
        fc[:, half, 0, :] = C[:, half * 128:(half + 1) * 128]
        fc[:, half, 1, :] = -S[:, half * 128:(half + 1) * 128]
    t["fc"] = fc.reshape(256, 512).astype(np.float32).astype(bf)
    t["pc"] = np.concatenate([C, S], axis=1).astype(np.float32).astype(bf)
    n = np.arange(128)[:, None].astype(np.float64)
    k = np.arange(128)[None, :].astype(np.float64)
    a = 2 * np.pi * n * k / 128.0
    C1, S1 = np.cos(a), np.sin(a)
    t["t1a"] = (np.concatenate([C1, -S1], axis=1) / 8.0).astype(np.float32).astype(bf)
    t["t1b"] = (np.concatenate([S1, C1], axis=1) / 8.0).astype(np.float32).astype(bf)
    t["t3"] = (np.concatenate([C1, S1], axis=1) / 16.0).astype(np.float32).astype(bf)
    tw = 2 * np.pi * n * k / 16384.0
    Tr, Ti = np.cos(tw), -np.sin(tw)
    t["tw1"] = np.concatenate([Tr, Ti], axis=1).astype(np.float32)
    t["tw2"] = np.concatenate([Ti, Tr], axis=1).astype(np.float32)
    return t


class Ctx:
    def __init__(self, nc, P):
        self.nc, self.P = nc, P
        self.n = 0

    def sb(self, shape, dt, name=None):
        self.n += 1
        return self.nc.alloc_sbuf_tensor((name or "sb") + f"_s{self.n}", list(shape), dt)

    def ps(self, shape, dt, name=None):
        self.n += 1
        return self.nc.alloc_psum_tensor((name or "ps") + f"_p{self.n}", list(shape), dt)

    def din(self, name, shape, dt):
        return self.nc.dram_tensor(name, list(shape), dt, kind="ExternalInput").ap()

    def dout(self, name, shape, dt):
        return self.nc.dram_tensor(name, list(shape), dt, kind="ExternalOutput").ap()


class Rot:
    def __init__(self, bufs, toks):
        self.bufs, self.toks, self.i = bufs, toks, 0

    def next(self):
        j = self.i % len(self.bufs)
        self.i += 1
        return self.bufs[j], self.toks[j]


def mk_rot(cx, n, shape, dt, name, psum=False):
    bufs = [(cx.ps if psum else cx.sb)(shape, dt, f"{name}{i}") for i in range(n)]
    return Rot(bufs, cx.P.toks(name, n))


def emit_rstd(cx, R, ss, t_ss, eps, n=D):
    P = cx.P
    rs, t_rs = R["rs"].next()
    t_mid = P.tok("rsmid")
    P.op("vector", lambda e: e.tensor_scalar(out=rs[:, 0:1], in0=ss[:, 0:1], scalar1=1.0 / n, scalar2=eps,
                                             op0=ALU.mult, op1=ALU.add), reads=[t_ss], writes=[t_rs])
    P.op("scalar", lambda e: e.activation(out=rs[:, 0:1], in_=rs[:, 0:1], func=AF.Sqrt),
         reads=[t_rs], writes=[t_rs])
    P.op("vector", lambda e: e.reciprocal(out=rs[:, 1:2], in_=rs[:, 0:1]), reads=[t_rs], writes=[t_rs])
    return rs, t_rs


def emit_norm_transpose(cx, R, src_rows, ntiles, A_bc, B_bc, t_AB, hT, t_hT, eps=1e-6):
    P = cx.P
    for i in range(ntiles):
        xt, t_x = R["xt"].next()
        P.dma("sync", xt[:, :], src_rows(i), writes=[t_x])
        sq, t_sq = R["sq"].next()
        ss, t_ss = R["ss"].next()
        P.op("scalar", lambda e, xt=xt, sq=sq, ss=ss: e.activation(
            out=sq[:, :], in_=xt[:, :], func=AF.Square, accum_out=ss[:, 0:1]),
            reads=[t_x], writes=[t_sq, t_ss])
        rs, t_rs = emit_rstd(cx, R, ss, t_ss, eps)
        tt, t_tt = R["tt"].next()
        P.op("vector", lambda e, xt=xt, rs=rs, tt=tt: e.scalar_tensor_tensor(
            out=tt[:, :], in0=xt[:, :], scalar=rs[:, 1:2], in1=A_bc[:, :], op0=ALU.mult, op1=ALU.mult),
            reads=[t_x, t_rs, t_AB], writes=[t_tt])
        hb, t_hb = R["hb"].next()
        P.op("vector", lambda e, tt=tt, hb=hb: e.tensor_tensor(
            out=hb[:, :], in0=tt[:, :], in1=B_bc[:, :], op=ALU.add),
            reads=[t_tt, t_AB], writes=[t_hb])
        for half in range(2):
            pT, t_pT = R["pT"].next()

            def tr(e, hb=hb, pT=pT, half=half):
                ins = None
                for j in range(8):
                    kc = half * 8 + j
                    ins = e.transpose(pT[:, j * 128:(j + 1) * 128], hb[:, kc * 128:(kc + 1) * 128],
                                      R["ident"][:, :])
                return ins
            P.op("tensor", tr, reads=[t_hb, R["t_ident"]], writes=[t_pT])
            eng = "scalar" if half == 0 else "vector"

            def ev(e, pT=pT, half=half, i=i, eng=eng):
                o = hT[:, half * 8:(half + 1) * 8, i * 128:(i + 1) * 128]
                s = pT[:, :].rearrange("p (j t) -> p j t", j=8)
                if eng == "scalar":
                    return e.copy(out=o, in_=s)
                return e.tensor_copy(out=o, in_=s)
            P.op(eng, ev, reads=[t_pT], pwrites=[t_hT])


def load_bc(cx, dst, t_dst, row_ap, eng="sync"):
    cx.P.dma(eng, dst[:, :], row_ap.broadcast_to([128, D]), pwrites=[t_dst])


ST = 512


def std_rots(cx, acc_n=3):
    R = {}
    R["xt"] = mk_rot(cx, 2, [128, D], F32, "xt")
    R["sq"] = mk_rot(cx, 1, [128, D], BF16, "sq")
    R["ss"] = mk_rot(cx, 2, [128, 1], F32, "ss")
    R["rs"] = mk_rot(cx, 2, [128, 2], F32, "rs")
    R["tt"] = mk_rot(cx, 2, [128, D], F32, "tt")
    R["hb"] = mk_rot(cx, 2, [128, D], BF16, "hb")
    R["pT"] = mk_rot(cx, 2, [128, 1024], BF16, "pT", psum=True)
    if acc_n:
        R["acc"] = mk_rot(cx, acc_n, [128, 512], F32, "acc", psum=True)
    R["wsl"] = mk_rot(cx, 2, [128, KC, 512], BF16, "wsl")
    return R


def load_ident(cx, R, ident_d):
    R["ident"] = cx.sb([128, 128], BF16, "ident")
    R["t_ident"] = cx.P.tok("ident")
    cx.P.dma("sync", R["ident"][:, :], ident_d, writes=[R["t_ident"]])


def load_slab(cx, R, w_d, c0, ncols=512):
    wsl, t_w = R["wsl"].next()
    cx.P.dma("gpsimd", wsl[:, :, 0:ncols], w_d[:, c0:c0 + ncols].rearrange("(kc p) n -> p kc n", p=128),
             writes=[t_w])
    return wsl, t_w


def build_four_a(T=4096, NCTX=256):
    nc = bass.Bass("TRN2", target_bir_lowering=False)
    P = Prog(nc)
    cx = Ctx(nc, P)
    x_d = cx.din("x", [T, D], F32)
    ctx_d = cx.din("ctx", [NCTX, D], F32)
    vecs = cx.din("vecs", [6, D], F32)
    w_in = cx.din("w_in", [D, 2 * D], F32)
    w_out = cx.din("w_out", [D, D], F32)
    ident_d = cx.din("ident", [128, 128], BF16)
    fc_d = cx.din("fc", [256, 512], BF16)
    pc_d = cx.din("pc", [256, 512], BF16)
    V_d = cx.dout("V", [T, 8 * 512], BF16)
    szT_d = cx.dout("szT", [D, T], F32)
    ctx1_d = cx.dout("ctx1", [NCTX, D], F32)

    R = std_rots(cx)
    load_ident(cx, R, ident_d)
    fc = cx.sb([128, 2, 512], BF16, "fc")
    pc = cx.sb([128, 2, 512], BF16, "pc")
    t_tab = P.tok("tab")
    P.dma("sync", fc[:, :, :], fc_d.rearrange("(cc p) n -> p cc n", p=128), pwrites=[t_tab])
    P.dma("sync", pc[:, :, :], pc_d.rearrange("(cc p) n -> p cc n", p=128), pwrites=[t_tab])
    A_bc = cx.sb([128, D], F32, "A_bc")
    B_bc = cx.sb([128, D], F32, "B_bc")
    G_bc = cx.sb([128, D], F32, "G_bc")
    t_AB, t_G = P.tok("AB"), P.tok("G")
    hT = cx.sb([128, KC, ST], BF16, "hT")
    uT = cx.sb([128, KC, ST], BF16, "uT")
    t_hT, t_uT = P.tok("hT"), P.tok("uT")
    szst = mk_rot(cx, 3, [128, 512], F32, "szst")
    Vt = mk_rot(cx, 2, [128, 8 * 512], BF16, "Vt")
    pv = mk_rot(cx, 2, [128, 512], F32, "pv", psum=True)
    szc = cx.sb([128, KC, NCTX], F32, "szc")
    Vc = cx.sb([128, 2, 8 * 512], BF16, "Vc")
    t_szc, t_Vc = P.tok("szc"), P.tok("Vc")

    def set_AB(scale_row, shift_row):
        g_bc, t_g = R["tt"].next()
        s_bc, t_s = R["tt"].next()
        P.dma("sync", g_bc[:, :], vecs[0:1, :].broadcast_to([128, D]), writes=[t_g])
        P.dma("sync", s_bc[:, :], vecs[scale_row:scale_row + 1, :].broadcast_to([128, D]), writes=[t_s])
        P.dma("sync", B_bc[:, :], vecs[shift_row:shift_row + 1, :].broadcast_to([128, D]), pwrites=[t_AB])
        P.op("vector", lambda e: e.scalar_tensor_tensor(
            out=A_bc[:, :], in0=s_bc[:, :], scalar=1.0, in1=g_bc[:, :], op0=ALU.add, op1=ALU.mult),
            reads=[t_g, t_s], pwrites=[t_AB])

    def project(ntok, ev_fn):
        for s in range(8):
            wsl, t_w = load_slab(cx, R, w_in, s * 512)
            for j in range(4):
                acc, t_acc = R["acc"].next()

                def mm(e, wsl=wsl, j=j, acc=acc):
                    ins = None
                    for kc in range(KC):
                        ins = e.matmul(acc[:, 0:ntok], lhsT=wsl[:, kc, j * 128:(j + 1) * 128],
                                       rhs=hT[:, kc, 0:ntok], start=(kc == 0), stop=(kc == KC - 1))
                    return ins
                P.op("tensor", mm, reads=[t_w, t_hT], writes=[t_acc])
                ev_fn(4 * s + j, acc, t_acc)

    def chan_dft(i, dst_ap, dst_tok, g):
        pvb, t_pv = pv.next()

        def mm(e):
            ins = None
            for cc in range(2):
                ins = e.matmul(pvb[:, :], lhsT=uT[:, 2 * g + cc, i * 128:(i + 1) * 128],
                               rhs=fc[:, cc, :], start=(cc == 0), stop=(cc == 1))
            return ins
        P.op("tensor", mm, reads=[t_uT, t_tab], writes=[t_pv])
        if g % 2 == 0:
            P.op("scalar", lambda e: e.copy(out=dst_ap, in_=pvb[:, :]), reads=[t_pv], pwrites=[dst_tok])
        else:
            P.op("vector", lambda e: e.tensor_copy(out=dst_ap, in_=pvb[:, :]), reads=[t_pv], pwrites=[dst_tok])

    set_AB(4, 3)
    P.dma("sync", G_bc[:, :], vecs[5:6, :].broadcast_to([128, D]), writes=[t_G])
    emit_norm_transpose(cx, R, lambda i: ctx_d[i * 128:(i + 1) * 128, :], NCTX // 128, A_bc, B_bc, t_AB, hT, t_hT)

    def ev_ctx(ci, acc, t_acc):
        if ci < KC:
            P.op("vector", lambda e: e.tensor_copy(out=uT[:, ci, 0:NCTX], in_=acc[:, 0:NCTX]),
                 reads=[t_acc], pwrites=[t_uT])
        else:
            P.op("scalar", lambda e: e.activation(out=szc[:, ci - KC, :], in_=acc[:, 0:NCTX], func=AF.Silu),
                 reads=[t_acc], pwrites=[t_szc])
    project(NCTX, ev_ctx)
    for i in range(NCTX // 128):
        for g in range(8):
            chan_dft(i, Vc[:, i, g * 512:(g + 1) * 512], t_Vc, g)
    for q in range(KC):
        acc, t_acc = R["acc"].next()

        def mm(e, q=q, acc=acc):
            ins = None
            n = 0
            for i in range(2):
                for ri in range(2):
                    ins = e.matmul(acc[:, 0:NCTX], lhsT=Vc[:, i, q * 256 + ri * 128: q * 256 + (ri + 1) * 128],
                                   rhs=pc[:, i, ri * 256:(ri + 1) * 256], start=(n == 0), stop=(n == 3))
                    n += 1
            return ins
        P.op("tensor", mm, reads=[t_Vc, t_tab], writes=[t_acc])
        P.op("vector", lambda e, q=q, acc=acc: e.tensor_tensor(
            out=hT[:, q, 0:NCTX], in0=acc[:, 0:NCTX], in1=szc[:, q, :], op=ALU.mult),
            reads=[t_acc, t_szc], pwrites=[t_hT])
    cts = []
    for i in range(NCTX // 128):
        xt, t_x = R["xt"].next()
        P.dma("sync", xt[:, :], ctx_d[i * 128:(i + 1) * 128, :], writes=[t_x])
        ot, t_o = R["tt"].next()
        cts.append((xt, t_x, ot, t_o))
    for s in range(4):
        wsl, t_w = load_slab(cx, R, w_out, s * 512)
        for i in range(NCTX // 128):
            xt, t_x, ot, t_o = cts[i]
            acc, t_acc = R["acc"].next()

            def mm(e, wsl=wsl, i=i, acc=acc):
                ins = None
                for kc in range(KC):
                    ins = e.matmul(acc[:, :], lhsT=hT[:, kc, i * 128:(i + 1) * 128], rhs=wsl[:, kc, :],
                                   start=(kc == 0), stop=(kc == KC - 1))
                return ins
            P.op("tensor", mm, reads=[t_w, t_hT], writes=[t_acc])
            sl = slice(s * 512, (s + 1) * 512)

            def ep(e, acc=acc, ot=ot, xt=xt, sl=sl):
                e.tensor_tensor(out=ot[:, sl], in0=acc[:, :], in1=G_bc[:, sl], op=ALU.mult)
                return e.tensor_tensor(out=ot[:, sl], in0=ot[:, sl], in1=xt[:, sl], op=ALU.add)
            P.op("vector", ep, reads=[t_acc, t_G, t_x], pwrites=[t_o])
    for i in range(NCTX // 128):
        xt, t_x, ot, t_o = cts[i]
        P.dma("sync", ctx1_d[i * 128:(i + 1) * 128, :], ot[:, :], reads=[t_o], writes=[P.tok("ctx1")])

    set_AB(2, 1)
    for st in range(T // ST):
        emit_norm_transpose(cx, R, lambda i, st=st: x_d[st * ST + i * 128: st * ST + (i + 1) * 128, :],
                            ST // 128, A_bc, B_bc, t_AB, hT, t_hT)

        def ev_lat(ci, acc, t_acc, st=st):
            if ci < KC:
                P.op("vector", lambda e: e.tensor_copy(out=uT[:, ci, :], in_=acc[:, :]),
                     reads=[t_acc], pwrites=[t_uT])
            else:
                sb, t_sb = szst.next()
                P.op("scalar", lambda e: e.activation(out=sb[:, :], in_=acc[:, :], func=AF.Silu),
                     reads=[t_acc], writes=[t_sb])
                c0 = (ci - KC) * 128
                P.dma("sync", szT_d[c0:c0 + 128, st * ST:(st + 1) * ST], sb[:, :], reads=[t_sb],
                      writes=[P.tok("szT")], semtok=t_sb)
        project(ST, ev_lat)
        for i in range(ST // 128):
            vb, t_vb = Vt.next()
            for g in range(8):
                chan_dft(i, vb[:, g * 512:(g + 1) * 512], t_vb, g)
            r0 = st * ST + i * 128
            P.dma("sync", V_d[r0:r0 + 128, :], vb[:, :], reads=[t_vb], writes=[P.tok("V")], semtok=t_vb)
    P.emit()
    return nc


def build_four_b(NSB=4):
    nc = bass.Bass("TRN2", target_bir_lowering=False)
    P = Prog(nc)
    cx = Ctx(nc, P)
    V_d = cx.din("V2", [16384, NSB * 256], BF16)
    t1a_d = cx.din("t1a", [128, 256], BF16)
    t1b_d = cx.din("t1b", [128, 256], BF16)
    t3_d = cx.din("t3", [128, 256], BF16)
    tw1_d = cx.din("tw1", [128, 256], F32)
    tw2_d = cx.din("tw2", [128, 256], F32)
    MT_d = cx.dout("MT", [NSB * 128, 16384], F32)

    t1a = cx.sb([128, 256], BF16, "t1a")
    t1b = cx.sb([128, 256], BF16, "t1b")
    t3 = cx.sb([128, 256], BF16, "t3")
    tw1 = cx.sb([128, 256], F32, "tw1")
    tw2 = cx.sb([128, 256], F32, "tw2")
    t_tab = P.tok("tab")
    for dst, src in ((t1a, t1a_d), (t1b, t1b_d), (t3, t3_d), (tw1, tw1_d), (tw2, tw2_d)):
        P.dma("sync", dst[:, :], src, pwrites=[t_tab])
    XM = cx.sb([128, 16384], F32, "XM")
    t_XM = P.tok("XM")
    X = XM[:, :].bitcast(BF16).rearrange("p (n r c) -> p n r c", n=128, r=2, c=128)
    M = XM[:, :].rearrange("p (k2 k1) -> p k2 k1", k2=128, k1=128)
    Bt = cx.sb([128, 2, 128, 128], BF16, "Bt")
    t_B = P.tok("B")
    p1r = mk_rot(cx, 2, [128, 2, 256], F32, "p1", psum=True)
    p3r = mk_rot(cx, 2, [128, 4, 128], F32, "p3", psum=True)
    P1r = mk_rot(cx, 2, [128, 2, 256], F32, "P1")
    P2r = mk_rot(cx, 2, [128, 2, 256], F32, "P2")
    Vv = V_d.rearrange("(n1 n2) c -> n1 n2 c", n1=128, n2=128)

    for sb in range(NSB):
        for q in range(8):
            P.dma("sync", X[:, q * 16:(q + 1) * 16, :, :],
                  Vv[:, q * 16:(q + 1) * 16, sb * 256:(sb + 1) * 256].rearrange("p n (r c) -> p n r c", r=2),
                  pwrites=[t_XM])
        for cp in range(64):
            p1, t_p1 = p1r.next()

            def mm1(e, cp=cp, p1=p1):
                ins = None
                for cj in range(2):
                    c = 2 * cp + cj
                    e.matmul(p1[:, cj, :], lhsT=X[:, :, 0, c], rhs=t1a[:, :], start=True, stop=False)
                    ins = e.matmul(p1[:, cj, :], lhsT=X[:, :, 1, c], rhs=t1b[:, :], start=False, stop=True)
                return ins
            P.op("tensor", mm1, reads=[t_XM, t_tab], writes=[t_p1])
            P1, t_P1 = P1r.next()
            P2, t_P2 = P2r.next()
            P.op("vector", lambda e, p1=p1, P1=P1: e.tensor_tensor(
                out=P1[:, :, :], in0=p1[:, :, :], in1=tw1[:, :].unsqueeze(1).broadcast_to([128, 2, 256]),
                op=ALU.mult), reads=[t_p1, t_tab], writes=[t_P1])
            P.op("vector", lambda e, p1=p1, P2=P2: e.tensor_tensor(
                out=P2[:, :, :], in0=p1[:, :, :], in1=tw2[:, :].unsqueeze(1).broadcast_to([128, 2, 256]),
                op=ALU.mult), reads=[t_p1, t_tab], writes=[t_P2])
            P.op("gpsimd", lambda e, cp=cp, P1=P1: e.tensor_tensor(
                out=Bt[:, 0, :, 2 * cp:2 * cp + 2], in0=P1[:, :, 0:128].rearrange("p c k -> p k c"),
                in1=P1[:, :, 128:256].rearrange("p c k -> p k c"), op=ALU.subtract),
                reads=[t_P1], pwrites=[t_B])
            P.op("gpsimd", lambda e, cp=cp, P2=P2: e.tensor_tensor(
                out=Bt[:, 1, :, 2 * cp:2 * cp + 2], in0=P2[:, :, 0:128].rearrange("p c k -> p k c"),
                in1=P2[:, :, 128:256].rearrange("p c k -> p k c"), op=ALU.add),
                reads=[t_P2], pwrites=[t_B])
        for kg in range(32):
            p3, t_p3 = p3r.next()

            def mm3(e, kg=kg, p3=p3):
                ins = None
                for kk in range(4):
                    k1 = 4 * kg + kk
                    e.matmul(p3[:, kk, :], lhsT=Bt[:, 0, k1, :], rhs=t3[:, 0:128], start=True, stop=False)
                    ins = e.matmul(p3[:, kk, :], lhsT=Bt[:, 1, k1, :], rhs=t3[:, 128:256], start=False, stop=True)
                return ins
            P.op("tensor", mm3, reads=[t_B, t_tab], writes=[t_p3])
            o = M[:, :, 4 * kg:4 * kg + 4]
            s = p3[:, :, :].rearrange("p a k -> p k a")
            if kg % 2 == 0:
                P.op("scalar", lambda e, o=o, s=s: e.copy(out=o, in_=s), reads=[t_p3], pwrites=[t_XM])
            else:
                P.op("vector", lambda e, o=o, s=s: e.tensor_copy(out=o, in_=s), reads=[t_p3], pwrites=[t_XM])
        for q in range(4):
            P.dma("sync", MT_d[sb * 128:(sb + 1) * 128, q * 4096:(q + 1) * 4096], XM[:, q * 4096:(q + 1) * 4096],
                  reads=[t_XM], writes=[P.tok("MT")], semtok=t_XM)
    P.emit()
    return nc


STC = 256


def load_wout_resident(cx, w_out_d):
    P = cx.P
    Wo = cx.sb([128, KC, D], BF16, "Wo")
    t_Wo = P.tok("Wo")
    for s in range(4):
        P.dma("gpsimd", Wo[:, :, s * 512:(s + 1) * 512],
              w_out_d[:, s * 512:(s + 1) * 512].rearrange("(kc p) n -> p kc n", p=128), pwrites=[t_Wo])
    return Wo, t_Wo


def emit_out_tile(cx, R, yT, t_yT, tcol, Wo, t_Wo, G_bc, t_G, x_rows, out_rows, final, FG_bc, t_FG, eps=1e-6):
    P = cx.P
    xt, t_x = R["xt"].next()
    P.dma("sync", xt[:, :], x_rows, writes=[t_x])
    ot, t_o = R["tt"].next()
    for s in range(4):
        acc, t_acc = R["acc"].next()

        def mm(e, s=s, acc=acc):
            ins = None
            for kc in range(KC):
                ins = e.matmul(acc[:, :], lhsT=yT[:, kc, tcol:tcol + 128], rhs=Wo[:, kc, s * 512:(s + 1) * 512],
                               start=(kc == 0), stop=(kc == KC - 1))
            return ins
        P.op("tensor", mm, reads=[t_Wo, t_yT], writes=[t_acc])
        sl = slice(s * 512, (s + 1) * 512)

        def ep(e, acc=acc, sl=sl):
            e.tensor_tensor(out=ot[:, sl], in0=acc[:, :], in1=G_bc[:, sl], op=ALU.mult)
            return e.tensor_tensor(out=ot[:, sl], in0=ot[:, sl], in1=xt[:, sl], op=ALU.add)
        P.op("vector", ep, reads=[t_acc, t_G, t_x], pwrites=[t_o])
    if final:
        sq, t_sq = R["sq"].next()
        ss, t_ss = R["ss"].next()
        P.op("scalar", lambda e: e.activation(out=sq[:, :], in_=ot[:, :], func=AF.Square, accum_out=ss[:, 0:1]),
             reads=[t_o], writes=[t_sq, t_ss])
        rs, t_rs = emit_rstd(cx, R, ss, t_ss, eps)
        P.op("vector", lambda e: e.scalar_tensor_tensor(
            out=ot[:, :], in0=ot[:, :], scalar=rs[:, 1:2], in1=FG_bc[:, :], op0=ALU.mult, op1=ALU.mult),
            reads=[t_rs, t_FG], writes=[t_o])
    P.dma("sync", out_rows, ot[:, :], reads=[t_o], writes=[P.tok("xo")], semtok=t_o)


def build_four_c(T=4096, final=False):
    nc = bass.Bass("TRN2", target_bir_lowering=False)
    P = Prog(nc)
    cx = Ctx(nc, P)
    MT_d = cx.din("MT", [D, T], F32)
    szT_d = cx.din("szT", [D, T], F32)
    x_d = cx.din("x", [T, D], F32)
    vecs = cx.din("vecs", [2, D], F32)
    w_out = cx.din("w_out", [D, D], F32)
    xo_d = cx.dout("xo", [T, D], F32)
    R = {}
    R["xt"] = mk_rot(cx, 2, [128, D], F32, "xt")
    R["tt"] = mk_rot(cx, 2, [128, D], F32, "tt")
    R["acc"] = mk_rot(cx, 4, [128, 512], F32, "acc", psum=True)
    R["sq"] = mk_rot(cx, 1, [128, D], BF16, "sq")
    R["ss"] = mk_rot(cx, 2, [128, 1], F32, "ss")
    R["rs"] = mk_rot(cx, 2, [128, 2], F32, "rs")
    G_bc = cx.sb([128, D], F32, "G_bc")
    FG_bc = cx.sb([128, D], F32, "FG_bc")
    t_G, t_FG = P.tok("G"), P.tok("FG")
    P.dma("sync", G_bc[:, :], vecs[0:1, :].broadcast_to([128, D]), writes=[t_G])
    P.dma("sync", FG_bc[:, :], vecs[1:2, :].broadcast_to([128, D]), writes=[t_FG])
    Wo, t_Wo = load_wout_resident(cx, w_out)
    Ms = mk_rot(cx, 1, [128, KC, STC], F32, "Ms")
    Zs = mk_rot(cx, 1, [128, KC, STC], F32, "Zs")
    yTr = mk_rot(cx, 2, [128, KC, STC], BF16, "yT")
    for st in range(T // STC):
        ms, t_ms = Ms.next()
        zs, t_zs = Zs.next()
        tsl = slice(st * STC, (st + 1) * STC)
        P.dma("sync", ms[:, :, :], MT_d[:, tsl].rearrange("(kc p) t -> p kc t", p=128), writes=[t_ms])
        P.dma("sync", zs[:, :, :], szT_d[:, tsl].rearrange("(kc p) t -> p kc t", p=128), writes=[t_zs])
        yT, t_yT = yTr.next()
        P.op("vector", lambda e, ms=ms, zs=zs, yT=yT: e.tensor_tensor(
            out=yT[:, 0:8, :], in0=ms[:, 0:8, :], in1=zs[:, 0:8, :], op=ALU.mult),
            reads=[t_ms, t_zs], pwrites=[t_yT])
        P.op("gpsimd", lambda e, ms=ms, zs=zs, yT=yT: e.tensor_tensor(
            out=yT[:, 8:16, :], in0=ms[:, 8:16, :], in1=zs[:, 8:16, :], op=ALU.mult),
            reads=[t_ms, t_zs], pwrites=[t_yT])
        for i in range(STC // 128):
            r0 = st * STC + i * 128
            emit_out_tile(cx, R, yT, t_yT, i * 128, Wo, t_Wo, G_bc, t_G, x_d[r0:r0 + 128, :],
                          xo_d[r0:r0 + 128, :], final, FG_bc, t_FG)
    P.emit()
    return nc


_NC_CACHE = {}


def _get_nc(name, fn, *a, **kw):
    key = (name, a, tuple(sorted(kw.items())))
    if key not in _NC_CACHE:
        _NC_CACHE[key] = fn(*a, **kw)
    return _NC_CACHE[key]


def _launch(nc, in_maps):
    res = run_bass_kernel_spmd(nc, in_maps, core_ids=list(range(NCORES)))
    return res.results


def run_fourier_layer(x, ctx, mod_l, g_l, w_in, w_out, tabs, final_g=None, with_ctx=True):
    B, N, _ = x.shape
    TP = N // 4
    sh, sc, gt = mod_l[:, 0:D], mod_l[:, D:2 * D], mod_l[:, 2 * D:3 * D]
    nca = _get_nc("four_a", build_four_a)
    in_maps = []
    for k in range(NCORES):
        b, i = k // 4, k % 4
        vecs = np.stack([g_l, sh[b], sc[b], sh[2], sc[2], gt[2]]).astype(np.float32)
        in_maps.append({"x": np.ascontiguousarray(x[b, i * TP:(i + 1) * TP]), "ctx": np.ascontiguousarray(ctx[b]),
                        "vecs": vecs, "w_in": w_in, "w_out": w_out, "ident": tabs["ident"],
                        "fc": tabs["fc"], "pc": tabs["pc"]})
    ra = _launch(nca, in_maps)
    ctx1 = np.stack([ra[0]["ctx1"], ra[4]["ctx1"]])
    ncb = _get_nc("four_b", build_four_b)
    in_maps = []
    for k in range(NCORES):
        b, j = k // 4, k % 4
        V2 = np.concatenate([ra[b * 4 + i]["V"][:, j * 1024:(j + 1) * 1024] for i in range(4)], axis=0)
        in_maps.append({"V2": np.ascontiguousarray(V2), "t1a": tabs["t1a"], "t1b": tabs["t1b"], "t3": tabs["t3"],
                        "tw1": tabs["tw1"], "tw2": tabs["tw2"]})
    rb = _launch(ncb, in_maps)
    ncc = _get_nc("four_c", build_four_c, final=final_g is not None)
    in_maps = []
    for k in range(NCORES):
        b, i = k // 4, k % 4
        MT = np.concatenate([rb[b * 4 + j]["MT"][:, i * TP:(i + 1) * TP] for j in range(4)], axis=0)
        vecs = np.stack([gt[b], final_g if final_g is not None else gt[b]]).astype(np.float32)
        in_maps.append({"MT": np.ascontiguousarray(MT), "szT": ra[k]["szT"],
                        "x": np.ascontiguousarray(x[b, i * TP:(i + 1) * TP]), "vecs": vecs, "w_out": w_out})
    rc = _launch(ncc, in_maps)
    xo = np.stack([np.concatenate([rc[b * 4 + i]["xo"] for i in range(4)], axis=0) for b in range(B)])
    return xo, ctx1


HALO = 15
CW = 31


def precast_weight(cx, w_d, ncols, name):
    P = cx.P
    wb = cx.nc.dram_tensor(name, [D, ncols], BF16).ap()
    t_wb = P.tok(name)
    step = 512
    for r in range(0, D, step):
        P.dma("gpsimd", wb[r:r + step, :], w_d[r:r + step, :], pwrites=[t_wb])
    return wb, t_wb


def load_slab_bf(cx, R, wb, t_wb, c0, ncols=512):
    wsl, t_w = R["wsl"].next()
    cx.P.dma("sync", wsl[:, :, 0:ncols], wb[:, c0:c0 + ncols].rearrange("(kc p) n -> p kc n", p=128),
             reads=[t_wb], writes=[t_w])
    return wsl, t_w


def build_conv_a(T=4096):
    nc = bass.Bass("TRN2", target_bir_lowering=False)
    P = Prog(nc)
    cx = Ctx(nc, P)
    TE = T + 2 * HALO
    NW = ST + 2 * HALO
    x_d = cx.din("xe", [TE + 128, D], F32)
    vecs = cx.din("vecs", [3, D], F32)
    pv_d = cx.din("pvec", [128, KC * (CW + 3)], F32)
    hm_d = cx.din("hm", [128, 2], F32)
    w_in = cx.din("w_in", [D, 3 * D], F32)
    ident_d = cx.din("ident", [128, 128], BF16)
    MT_d = cx.dout("MT", [D, T], F32)
    szT_d = cx.dout("szT", [D, T], F32)

    R = std_rots(cx, acc_n=0)
    R["acc"] = mk_rot(cx, 2, [128, 1024], F32, "acc2", psum=True)
    load_ident(cx, R, ident_d)
    wb, t_wb = precast_weight(cx, w_in, 3 * D, "w_in_bf")
    pvec = cx.sb([128, KC, CW + 3], F32, "pvec")
    hm = cx.sb([128, 2], F32, "hm")
    ones = cx.sb([128, 128], F32, "ones")
    t_c = P.tok("consts")
    P.dma("sync", pvec[:, :, :], pv_d.rearrange("p (k w) -> p k w", k=KC), pwrites=[t_c])
    P.dma("sync", hm[:, :], hm_d, pwrites=[t_c])
    P.op("vector", lambda e: e.memset(ones[:, :], 1.0), pwrites=[t_c])
    A_bc = cx.sb([128, D], F32, "A_bc")
    B_bc = cx.sb([128, D], F32, "B_bc")
    t_AB = P.tok("AB")
    g_bc, t_g = R["tt"].next()
    s_bc, t_s = R["tt"].next()
    P.dma("sync", g_bc[:, :], vecs[0:1, :].broadcast_to([128, D]), writes=[t_g])
    P.dma("sync", s_bc[:, :], vecs[2:3, :].broadcast_to([128, D]), writes=[t_s])
    P.dma("sync", B_bc[:, :], vecs[1:2, :].broadcast_to([128, D]), pwrites=[t_AB])
    P.op("vector", lambda e: e.scalar_tensor_tensor(
        out=A_bc[:, :], in0=s_bc[:, :], scalar=1.0, in1=g_bc[:, :], op0=ALU.add, op1=ALU.mult),
        reads=[t_g, t_s], pwrites=[t_AB])

    hT = cx.sb([128, KC, 640], BF16, "hT")
    t_hT = P.tok("hT")
    Gw = cx.sb([128, KC, NW + 2], BF16, "Gw")
    t_Gw = P.toks("Gw", KC)
    Yb = cx.sb([128, KC, ST], F32, "Yb")
    t_Y = P.toks("Y", KC)
    sig = mk_rot(cx, 4, [128, NW + 2], F32, "sig")
    ca1 = mk_rot(cx, 2, [128, ST], F32, "ca1")
    ca2 = mk_rot(cx, 2, [128, ST], F32, "ca2")
    y2r = mk_rot(cx, 2, [128, ST], F32, "y2")
    ptmp = cx.sb([128, ST], F32, "ptmp")
    stg = mk_rot(cx, 3, [128, ST], F32, "stg")
    lnt = mk_rot(cx, 2, [128, ST], F32, "lnt")
    st1 = cx.ps([128, 512], F32, "st1")
    st2 = cx.ps([128, 512], F32, "st2")
    t_st = P.tok("st")
    mean = cx.sb([128, ST], F32, "mean")
    rstd = cx.sb([128, ST], F32, "rstd")
    nmr = cx.sb([128, ST], F32, "nmr")
    t_ln = P.tok("ln")
    nst = T // ST

    def mm_pair(wsl, j, acc, n_extra):
        def mm(e):
            ins = None
            for kc in range(KC):
                ins = e.matmul(acc[:, 0:ST], lhsT=wsl[:, kc, j * 128:(j + 1) * 128], rhs=hT[:, kc, 0:ST],
                               start=(kc == 0), stop=(kc == KC - 1))
            for kc in range(KC):
                ins = e.matmul(acc[:, 512:512 + n_extra], lhsT=wsl[:, kc, j * 128:(j + 1) * 128],
                               rhs=hT[:, kc, ST:ST + n_extra], start=(kc == 0), stop=(kc == KC - 1))
            return ins
        return mm

    for st in range(nst):
        emit_norm_transpose(cx, R, lambda i, st=st: x_d[st * ST + i * 128: st * ST + (i + 1) * 128, :],
                            5, A_bc, B_bc, t_AB, hT, t_hT)
        for q in range(4):
            wg, t_wg = load_slab_bf(cx, R, wb, t_wb, D + q * 512)
            wa, t_wa = load_slab_bf(cx, R, wb, t_wb, q * 512)
            sigs = []
            for j in range(4):
                acc, t_acc = R["acc"].next()
                P.op("tensor", mm_pair(wg, j, acc, 2 * HALO), reads=[t_wg, t_hT], writes=[t_acc])
                sg, t_sg = sig.next()

                def sg_ev(e, acc=acc, sg=sg):
                    e.activation(out=sg[:, 0:ST], in_=acc[:, 0:ST], func=AF.Sigmoid)
                    return e.activation(out=sg[:, ST:NW], in_=acc[:, 512:512 + 2 * HALO], func=AF.Sigmoid)
                P.op("scalar", sg_ev, reads=[t_acc], writes=[t_sg])
                sigs.append((sg, t_sg))
            for j in range(4):
                kc = 4 * q + j
                acc, t_acc = R["acc"].next()
                P.op("tensor", mm_pair(wa, j, acc, 2 * HALO), reads=[t_wa, t_hT], writes=[t_acc])
                sg, t_sg = sigs[j]

                def g_ev(e, acc=acc, sg=sg, kc=kc, st=st):
                    e.tensor_tensor(out=Gw[:, kc, 0:ST], in0=acc[:, 0:ST], in1=sg[:, 0:ST], op=ALU.mult)
                    ins = e.tensor_tensor(out=Gw[:, kc, ST:NW], in0=acc[:, 512:512 + 2 * HALO], in1=sg[:, ST:NW],
                                          op=ALU.mult)
                    if st == 0:
                        ins = e.tensor_scalar(out=Gw[:, kc, 0:HALO], in0=Gw[:, kc, 0:HALO], scalar1=hm[:, 0:1],
                                              scalar2=None, op0=ALU.mult)
                    if st == nst - 1:
                        ins = e.tensor_scalar(out=Gw[:, kc, NW - HALO:NW], in0=Gw[:, kc, NW - HALO:NW],
                                              scalar1=hm[:, 1:2], scalar2=None, op0=ALU.mult)
                    return ins
                P.op("vector", g_ev, reads=[t_acc, t_sg, t_c], writes=[t_Gw[kc]])
                a1, t_a1 = ca1.next()
                a2, t_a2 = ca2.next()

                def taps(lo, hi, dst, with_bias, kc=kc):
                    def f(e):
                        ins = e.tensor_scalar(out=dst[:, :], in0=Gw[:, kc, lo:lo + ST], scalar1=pvec[:, kc, lo:lo + 1],
                                              scalar2=(pvec[:, kc, CW:CW + 1] if with_bias else None),
                                              op0=ALU.mult, **({"op1": ALU.add} if with_bias else {}))
                        for k in range(lo + 1, hi):
                            if with_bias:
                                ins = e.scalar_tensor_tensor(out=dst[:, :], in0=Gw[:, kc, k:k + ST],
                                                             scalar=pvec[:, kc, k:k + 1], in1=dst[:, :],
                                                             op0=ALU.mult, op1=ALU.add)
                            else:
                                e.tensor_scalar(out=ptmp[:, :], in0=Gw[:, kc, k:k + ST], scalar1=pvec[:, kc, k:k + 1],
                                                scalar2=None, op0=ALU.mult)
                                ins = e.tensor_tensor(out=dst[:, :], in0=dst[:, :], in1=ptmp[:, :], op=ALU.add)
                        return ins
                    return f
                P.op("vector", taps(0, 23, a1, True), reads=[t_Gw[kc], t_c], writes=[t_a1])
                P.op("gpsimd", taps(23, CW, a2, False), reads=[t_Gw[kc], t_c], writes=[t_a2])
                P.op("vector", lambda e, a1=a1, a2=a2, kc=kc: e.tensor_tensor(
                    out=Yb[:, kc, :], in0=a1[:, :], in1=a2[:, :], op=ALU.add),
                    reads=[t_a1, t_a2], writes=[t_Y[kc]])
                y2, t_y2 = y2r.next()
                P.op("scalar", lambda e, y2=y2, kc=kc: e.activation(out=y2[:, :], in_=Yb[:, kc, :], func=AF.Square),
                     reads=[t_Y[kc]], writes=[t_y2])

                def stat(e, y2=y2, kc=kc):
                    e.matmul(st1[:, :], lhsT=ones[:, :], rhs=Yb[:, kc, :], start=(kc == 0), stop=(kc == KC - 1))
                    return e.matmul(st2[:, :], lhsT=ones[:, :], rhs=y2[:, :], start=(kc == 0), stop=(kc == KC - 1))
                P.op("tensor", stat, reads=[t_Y[kc], t_y2, t_c] + ([t_ln] if kc == 0 else []),
                     **({"writes": [t_st]} if kc == 0 else {"pwrites": [t_st]}))
        for q in range(4):
            wz, t_wz = load_slab_bf(cx, R, wb, t_wb, 2 * D + q * 512)
            for j in range(4):
                acc, t_acc = R["acc"].next()

                def mmz(e, wz=wz, j=j, acc=acc):
                    ins = None
                    for kc in range(KC):
                        ins = e.matmul(acc[:, 0:ST], lhsT=wz[:, kc, j * 128:(j + 1) * 128],
                                       rhs=hT[:, kc, HALO:HALO + ST], start=(kc == 0), stop=(kc == KC - 1))
                    return ins
                P.op("tensor", mmz, reads=[t_wz, t_hT], writes=[t_acc])
                sb, t_sb = stg.next()
                P.op("scalar", lambda e, acc=acc, sb=sb: e.activation(out=sb[:, :], in_=acc[:, 0:ST], func=AF.Silu),
                     reads=[t_acc], writes=[t_sb])
                c0 = (4 * q + j) * 128
                P.dma("sync", szT_d[c0:c0 + 128, st * ST:(st + 1) * ST], sb[:, :], reads=[t_sb],
                      writes=[P.tok("szT")], semtok=t_sb)

        def ln1(e):
            e.tensor_scalar(out=mean[:, :], in0=st1[:, :], scalar1=1.0 / D, scalar2=None, op0=ALU.mult)
            e.tensor_tensor(out=nmr[:, :], in0=mean[:, :], in1=mean[:, :], op=ALU.mult)
            e.scalar_tensor_tensor(out=rstd[:, :], in0=st2[:, :], scalar=1.0 / D, in1=nmr[:, :],
                                   op0=ALU.mult, op1=ALU.subtract)
            return e.tensor_scalar(out=rstd[:, :], in0=rstd[:, :], scalar1=1e-6, scalar2=None, op0=ALU.add)
        P.op("vector", ln1, reads=[t_st], writes=[t_ln])
        P.op("scalar", lambda e: e.activation(out=rstd[:, :], in_=rstd[:, :], func=AF.Sqrt),
             reads=[t_ln], writes=[t_ln])

        def ln2(e):
            e.reciprocal(out=rstd[:, :], in_=rstd[:, :])
            return e.scalar_tensor_tensor(out=nmr[:, :], in0=mean[:, :], scalar=-1.0, in1=rstd[:, :],
                                          op0=ALU.mult, op1=ALU.mult)
        P.op("vector", ln2, reads=[t_ln], writes=[t_ln])
        for kc in range(KC):
            lt, t_lt = lnt.next()

            def nrm(e, kc=kc, lt=lt):
                e.tensor_tensor(out=lt[:, :], in0=Yb[:, kc, :], in1=rstd[:, :], op=ALU.mult)
                return e.tensor_tensor(out=lt[:, :], in0=lt[:, :], in1=nmr[:, :], op=ALU.add)
            P.op("vector" if kc % 2 == 0 else "gpsimd", nrm, reads=[t_Y[kc], t_ln], writes=[t_lt])
            sb, t_sb = stg.next()
            P.op("scalar", lambda e, kc=kc, lt=lt, sb=sb: e.activation(
                out=sb[:, :], in_=lt[:, :], func=AF.Silu, scale=pvec[:, kc, CW + 1:CW + 2],
                bias=pvec[:, kc, CW + 2:CW + 3]), reads=[t_lt, t_c], writes=[t_sb])
            P.dma("sync", MT_d[kc * 128:(kc + 1) * 128, st * ST:(st + 1) * ST], sb[:, :], reads=[t_sb],
                  writes=[P.tok("MT")], semtok=t_sb)
    P.emit()
    return nc


def run_generic_c(MTs, szTs, x, gate, w_out, final_g=None):
    B, N, _ = x.shape
    TP = N // 4
    ncc = _get_nc("four_c", build_four_c, final=final_g is not None)
    in_maps = []
    for k in range(NCORES):
        b, i = k // 4, k % 4
        vecs = np.stack([gate[b], final_g if final_g is not None else gate[b]]).astype(np.float32)
        in_maps.append({"MT": MTs[k], "szT": szTs[k], "x": np.ascontiguousarray(x[b, i * TP:(i + 1) * TP]),
                        "vecs": vecs, "w_out": w_out})
    rc = _launch(ncc, in_maps)
    return np.stack([np.concatenate([rc[b * 4 + i]["xo"] for i in range(4)], axis=0) for b in range(B)])


def run_conv_layer(x, mod_l, g_l, w_in, dw_w, dw_b, ln_g, ln_b, w_out, tabs, final_g=None):
    B, N, _ = x.shape
    TP = N // 4
    sh, sc, gt = mod_l[:, 0:D], mod_l[:, D:2 * D], mod_l[:, 2 * D:3 * D]
    nca = _get_nc("conv_a", build_conv_a)
    pv = np.concatenate([dw_w.T, dw_b[:, None], ln_g[:, None], ln_b[:, None]], axis=1)
    pv = np.ascontiguousarray(pv.reshape(KC, 128, CW + 3).transpose(1, 0, 2)).reshape(128, KC * (CW + 3))
    in_maps = []
    for k in range(NCORES):
        b, i = k // 4, k % 4
        xe = np.zeros((TP + 2 * HALO + 128, D), np.float32)
        lo, hi = i * TP - HALO, (i + 1) * TP + HALO
        slo, shi = max(lo, 0), min(hi, N)
        xe[slo - lo: shi - lo] = x[b, slo:shi]
        hm = np.zeros((128, 2), np.float32)
        hm[:, 0] = 1.0 if lo >= 0 else 0.0
        hm[:, 1] = 1.0 if hi <= N else 0.0
        vecs = np.stack([g_l, sh[b], sc[b]]).astype(np.float32)
        in_maps.append({"xe": xe, "vecs": vecs, "pvec": pv.astype(np.float32), "hm": hm, "w_in": w_in,
                        "ident": tabs["ident"]})
    ra = _launch(nca, in_maps)
    return run_generic_c([r["MT"] for r in ra], [r["szT"] for r in ra], x, gt, w_out, final_g)


AW = 768


def attn_tables(n_tokens=16384):
    bf = _bf16()
    pos = np.arange(n_tokens)
    row = (pos // 64).astype(np.float32)
    col = (pos % 64).astype(np.float32)
    inv = (10000.0 ** (-np.arange(16, dtype=np.float32) / 16.0)).astype(np.float32)
    ang_r = row[None, :] * inv[:, None]
    ang_c = col[None, :] * inv[:, None]
    ang = np.concatenate([ang_r, ang_r, ang_c, ang_c], axis=0)
    cos = np.cos(ang).astype(np.float32)
    sin = np.sin(ang).astype(np.float32)
    sign = np.concatenate([-np.ones(16), np.ones(16), -np.ones(16), np.ones(16)]).astype(np.float32)
    sinS = sin * sign[:, None]
    t = {"cos": np.concatenate([cos, cos], 0), "sin": np.concatenate([sinS, sinS], 0)}
    R = np.zeros((128, 128), np.float32)
    for hh in range(2):
        for dst in range(64):
            blk = dst // 16
            src = dst + 16 if blk % 2 == 0 else dst - 16
            R[hh * 64 + src, hh * 64 + dst] = 1.0
    t["R"] = R
    kk = np.arange(128)[:, None]
    qq = np.arange(128)[None, :]
    t["maskP"] = (kk >= qq).astype(np.float32).astype(bf)
    t["maskN"] = (kk <= qq).astype(np.float32).astype(bf)
    return t


def build_attn_a(T=4096, NCTX=256):
    nc = bass.Bass("TRN2", target_bir_lowering=False)
    P = Prog(nc)
    cx = Ctx(nc, P)
    TE = T + 256
    x_d = cx.din("xe", [TE, D], F32)
    ctx_d = cx.din("ctx", [NCTX, D], F32)
    vecs = cx.din("vecs", [5, D], F32)
    w_all = cx.din("w_all", [D, 5120], F32)
    ident_d = cx.din("ident", [128, 128], BF16)
    cos_d = cx.din("cos", [128, TE], F32)
    sin_d = cx.din("sin", [128, TE], F32)
    R_d = cx.din("R", [128, 128], F32)
    mk_d = cx.din("masks", [128, 4 * 128], BF16)
    sink_d = cx.din("sink", [128, 32], F32)
    MT_d = cx.dout("MT", [D, T], F32)
    szT_d = cx.dout("szT", [D, T], F32)

    R = std_rots(cx, acc_n=0)
    R["hb"] = mk_rot(cx, 1, [128, D], BF16, "hb1")
    pp = [cx.ps([128, 1024], F32, f"pp{i}") for i in range(3)]
    t_pp = [P.toks(f"pp{i}_", 2) for i in range(3)]
    load_ident(cx, R, ident_d)
    wb, t_wb = precast_weight(cx, w_all, 5120, "w_all_bf")
    Rm = cx.sb([128, 128], F32, "Rm")
    masks = cx.sb([128, 4, 128], BF16, "masks")
    sinkE = cx.sb([128, 32], F32, "sinkE")
    ones = cx.sb([128, 128], BF16, "ones")
    t_c = P.tok("consts")
    P.dma("sync", Rm[:, :], R_d, pwrites=[t_c])
    P.dma("sync", masks[:, :, :], mk_d.rearrange("p (a q) -> p a q", a=4), pwrites=[t_c])
    P.dma("sync", sinkE[:, :], sink_d, pwrites=[t_c])
    P.op("vector", lambda e: e.memset(ones[:, :], 1.0), pwrites=[t_c])
    P.op("scalar", lambda e: e.activation(out=sinkE[:, :], in_=sinkE[:, :], func=AF.Exp), reads=[t_c], writes=[t_c])
    A_bc = cx.sb([128, D], F32, "A_bc")
    B_bc = cx.sb([128, D], F32, "B_bc")
    t_AB = P.tok("AB")

    def set_AB(scale_row, shift_row):
        g_bc, t_g = R["tt"].next()
        s_bc, t_s = R["tt"].next()
        P.dma("sync", g_bc[:, :], vecs[0:1, :].broadcast_to([128, D]), writes=[t_g])
        P.dma("sync", s_bc[:, :], vecs[scale_row:scale_row + 1, :].broadcast_to([128, D]), writes=[t_s])
        P.dma("sync", B_bc[:, :], vecs[shift_row:shift_row + 1, :].broadcast_to([128, D]), pwrites=[t_AB])
        P.op("vector", lambda e: e.scalar_tensor_tensor(
            out=A_bc[:, :], in0=s_bc[:, :], scalar=1.0, in1=g_bc[:, :], op0=ALU.add, op1=ALU.mult),
            reads=[t_g, t_s], pwrites=[t_AB])

    hT = cx.sb([128, KC, AW], BF16, "hT")
    t_hT = P.tok("hT")
    KT = cx.sb([128, 4, AW], BF16, "KT")
    V2 = cx.sb([128, AW // 128, 512], BF16, "V2")
    QT = cx.sb([128, KC, ST], BF16, "QT")
    KTc = cx.sb([128, 4, NCTX], BF16, "KTc")
    Vc2 = cx.sb([128, NCTX // 128, 512], BF16, "Vc2")
    t_KT, t_V2, t_QT, t_KTc, t_Vc2 = P.tok("KT"), P.tok("V2"), P.tok("QT"), P.tok("KTc"), P.tok("Vc2")
    cosb = cx.sb([128, AW], F32, "cosb")
    sinb = cx.sb([128, AW], F32, "sinb")
    t_cs = P.tok("cs")
    qf = mk_rot(cx, 2, [128, AW], F32, "qf")
    r1 = mk_rot(cx, 2, [128, AW], F32, "r1")
    Pj = mk_rot(cx, 3, [128, 1024], BF16, "Pj")
    stg = mk_rot(cx, 3, [128, 512], F32, "stg")
    rden = cx.sb([128, 1024], F32, "rden")
    t_rden = P.tok("rden")

    def halves(i):
        return (pp[i][:, 0:512], t_pp[i][0]), (pp[i][:, 512:1024], t_pp[i][1])

    accs = Rot([halves(0)[0][0], halves(0)[1][0], halves(1)[0][0], halves(1)[1][0]],
               [t_pp[0][0], t_pp[0][1], t_pp[1][0], t_pp[1][1]])

    def proj_fm(wsl, t_w, j, col0, ntok):
        acc, t_acc = accs.next()

        def mm(e):
            ins = None
            for kc in range(KC):
                ins = e.matmul(acc[:, 0:ntok], lhsT=wsl[:, kc, j * 128:(j + 1) * 128],
                               rhs=hT[:, kc, col0:col0 + ntok], start=(kc == 0), stop=(kc == KC - 1))
            return ins
        P.op("tensor", mm, reads=[t_w, t_hT], writes=[t_acc])
        return acc, t_acc

    def rope_to(acc, t_acc, ntok, tcol0, dst_ap, dst_tok):
        q, t_q = qf.next()
        P.op("scalar", lambda e: e.copy(out=q[:, 0:ntok], in_=acc[:, 0:ntok]), reads=[t_acc], writes=[t_q])
        rp, t_rp = pp[2][:, 0:512], t_pp[2][0]
        P.op("tensor", lambda e: e.matmul(rp[:, 0:ntok], lhsT=Rm[:, :], rhs=q[:, 0:ntok], start=True, stop=True),
             reads=[t_q, t_c], writes=[t_rp])
        r, t_r = r1.next()
        P.op("vector", lambda e: e.tensor_tensor(out=r[:, 0:ntok], in0=rp[:, 0:ntok],
                                                 in1=sinb[:, tcol0:tcol0 + ntok], op=ALU.mult),
             reads=[t_rp, t_cs], writes=[t_r])
        P.op("gpsimd", lambda e: e.tensor_tensor(out=q[:, 0:ntok], in0=q[:, 0:ntok],
                                                 in1=cosb[:, tcol0:tcol0 + ntok], op=ALU.mult),
             reads=[t_cs], writes=[t_q])
        P.op("vector", lambda e: e.tensor_tensor(out=dst_ap, in0=q[:, 0:ntok], in1=r[:, 0:ntok], op=ALU.add),
             reads=[t_q, t_r], pwrites=[dst_tok])

    def v_proj(wsl, t_w, tile_col0, dst_ap, dst_tok):
        acc, t_acc = accs.next()

        def mm(e):
            ins = None
            for kc in range(KC):
                ins = e.matmul(acc[:, :], lhsT=hT[:, kc, tile_col0:tile_col0 + 128], rhs=wsl[:, kc, :],
                               start=(kc == 0), stop=(kc == KC - 1))
            return ins
        P.op("tensor", mm, reads=[t_w, t_hT], writes=[t_acc])
        P.op("scalar", lambda e: e.copy(out=dst_ap, in_=acc[:, :]), reads=[t_acc], pwrites=[dst_tok])

    set_AB(4, 3)
    emit_norm_transpose(cx, R, lambda i: ctx_d[i * 128:(i + 1) * 128, :], NCTX // 128, A_bc, B_bc, t_AB, hT, t_hT)
    wk, t_wk = load_slab_bf(cx, R, wb, t_wb, 2048)
    for j in range(4):
        acc, t_acc = proj_fm(wk, t_wk, j, 0, NCTX)
        P.op("vector", lambda e, acc=acc, j=j: e.tensor_copy(out=KTc[:, j, :], in_=acc[:, 0:NCTX]),
             reads=[t_acc], pwrites=[t_KTc])
    wv, t_wv = load_slab_bf(cx, R, wb, t_wb, 2560)
    for i in range(NCTX // 128):
        v_proj(wv, t_wv, i * 128, Vc2[:, i, :], t_Vc2)

    set_AB(2, 1)
    nst = T // ST
    for st in range(nst):
        w0 = st * ST
        P.dma("sync", cosb[:, :], cos_d[:, w0:w0 + AW], pwrites=[t_cs])
        P.dma("sync", sinb[:, :], sin_d[:, w0:w0 + AW], pwrites=[t_cs])
        emit_norm_transpose(cx, R, lambda i, w0=w0: x_d[w0 + i * 128: w0 + (i + 1) * 128, :],
                            AW // 128, A_bc, B_bc, t_AB, hT, t_hT)
        wk, t_wk = load_slab_bf(cx, R, wb, t_wb, 2048)
        for j in range(4):
            for (c0, n) in ((0, 512), (512, AW - 512)):
                acc, t_acc = proj_fm(wk, t_wk, j, c0, n)
                rope_to(acc, t_acc, n, c0, KT[:, j, c0:c0 + n], t_KT)
        wv, t_wv = load_slab_bf(cx, R, wb, t_wb, 2560)
        for i in range(AW // 128):
            v_proj(wv, t_wv, i * 128, V2[:, i, :], t_V2)
        for q in range(4):
            wq, t_wq = load_slab_bf(cx, R, wb, t_wb, q * 512)
            for j in range(4):
                acc, t_acc = proj_fm(wq, t_wq, j, 128, ST)
                rope_to(acc, t_acc, ST, 128, QT[:, 4 * q + j, :], t_QT)
        for q in range(4):
            wz, t_wz = load_slab_bf(cx, R, wb, t_wb, 3072 + q * 512)
            for j in range(4):
                acc, t_acc = proj_fm(wz, t_wz, j, 128, ST)
                sb, t_sb = stg.next()
                P.op("scalar", lambda e, acc=acc, sb=sb: e.activation(out=sb[:, :], in_=acc[:, :], func=AF.Silu),
                     reads=[t_acc], writes=[t_sb])
                c0 = (4 * q + j) * 128
                P.dma("sync", szT_d[c0:c0 + 128, st * ST:(st + 1) * ST], sb[:, :], reads=[t_sb],
                      writes=[P.tok("szT")], semtok=t_sb)
        for b in range(ST // 128):
            for h in range(4):
                Oa, t_O = pp[1], t_pp[1]
                Da, t_D = pp[2], t_pp[2]
                nj = 5
                for j in range(nj):
                    Sa, t_S = pp[0], t_pp[0]
                    if j < 3:
                        kt_ap = lambda half, j=j, b=b, h=h: KT[half * 64:(half + 1) * 64, h, (b + j) * 128:(b + j + 1) * 128]
                        v_ap = V2[:, b + j, h * 128:(h + 1) * 128]
                        rk, rv = t_KT, t_V2
                    else:
                        kt_ap = lambda half, j=j, h=h: KTc[half * 64:(half + 1) * 64, h, (j - 3) * 128:(j - 2) * 128]
                        v_ap = Vc2[:, j - 3, h * 128:(h + 1) * 128]
                        rk, rv = t_KTc, t_Vc2

                    def mm_s(e, kt_ap=kt_ap, b=b, h=h, Sa=Sa):
                        ins = None
                        for half in range(2):
                            ins = e.matmul(Sa[:, half * 512:(half + 1) * 512], lhsT=kt_ap(half),
                                           rhs=QT[half * 64:(half + 1) * 64, 4 * h:4 * h + 4, b * 128:(b + 1) * 128],
                                           start=True, stop=True)
                        return ins
                    P.op("tensor", mm_s, reads=[rk, t_QT], writes=[t_S[0], t_S[1]])
                    pj, t_pj = Pj.next()

                    def ex(e, Sa=Sa, pj=pj):
                        e.activation(out=pj[:, 0:512], in_=Sa[:, 0:512], func=AF.Exp, scale=0.125)
                        return e.activation(out=pj[:, 512:1024], in_=Sa[:, 512:1024], func=AF.Exp, scale=0.125)
                    P.op("scalar", ex, reads=[t_S[0], t_S[1]], writes=[t_pj])
                    mi = None
                    if j == 0:
                        mi = 2 if (st == 0 and b == 0) else 0
                    elif j == 2:
                        mi = 3 if (st == nst - 1 and b == ST // 128 - 1) else 1
                    if mi is not None:
                        P.op("gpsimd", lambda e, pj=pj, mi=mi: e.tensor_tensor(
                            out=pj[:, :].rearrange("p (g q) -> p g q", g=8),
                            in0=pj[:, :].rearrange("p (g q) -> p g q", g=8),
                            in1=masks[:, mi, :].unsqueeze(1).broadcast_to([128, 8, 128]), op=ALU.mult),
                            reads=[t_c], writes=[t_pj])

                    def mm_o(e, pj=pj, v_ap=v_ap, j=j, Oa=Oa, Da=Da):
                        ins = None
                        for half in range(2):
                            sl = slice(half * 512, (half + 1) * 512)
                            e.matmul(Oa[:, sl], lhsT=v_ap, rhs=pj[:, sl], start=(j == 0), stop=(j == nj - 1))
                            ins = e.matmul(Da[:, sl], lhsT=ones[:, :], rhs=pj[:, sl], start=(j == 0), stop=(j == nj - 1))
                        return ins
                    kw = {"writes": [t_O[0], t_O[1], t_D[0], t_D[1]]} if j == 0 else \
                         {"pwrites": [t_O[0], t_O[1], t_D[0], t_D[1]]}
                    P.op("tensor", mm_o, reads=[t_pj, rv, t_c], **kw)

                def fin(e, h=h, Da=Da):
                    e.tensor_tensor(out=rden[:, :].rearrange("p (g q) -> p g q", g=8),
                                    in0=Da[:, :].rearrange("p (g q) -> p g q", g=8),
                                    in1=sinkE[:, h * 8:(h + 1) * 8].unsqueeze(2).broadcast_to([128, 8, 128]),
                                    op=ALU.add)
                    return e.reciprocal(out=rden[:, :], in_=rden[:, :])
                P.op("vector", fin, reads=[t_D[0], t_D[1], t_c], writes=[t_rden])
                sb, t_sb = stg.next()

                def outn(e, Oa=Oa, sb=sb):
                    ins = None
                    for par in range(2):
                        ps_ = slice(par * 64, (par + 1) * 64)
                        cs_ = slice(par * 512, (par + 1) * 512)
                        ins = e.tensor_tensor(out=sb[ps_, :], in0=Oa[ps_, cs_], in1=rden[ps_, cs_], op=ALU.mult)
                    return ins
                P.op("vector", outn, reads=[t_O[0], t_O[1], t_rden], writes=[t_sb])
                q0 = st * ST + b * 128
                P.dma("sync", MT_d[h * 512:(h + 1) * 512, q0:q0 + 128].rearrange("(m p) q -> p m q", p=128),
                      sb[:, :].rearrange("p (m q) -> p m q", m=4), reads=[t_sb], writes=[P.tok("MT")], semtok=t_sb)
    P.emit()
    return nc


def run_attn_layer(x, ctx, mod_l, g_l, w_in, sink, w_out, tabs, atabs, final_g=None):
    B, N, _ = x.shape
    TP = N // 4
    bf = _bf16()
    sh, sc, gt = mod_l[:, 0:D], mod_l[:, D:2 * D], mod_l[:, 2 * D:3 * D]
    nca = _get_nc("attn_a", build_attn_a)
    wq, wk, wv, wz = w_in[:, 0:2048], w_in[:, 2048:2304], w_in[:, 2304:2560], w_in[:, 2560:4608]
    wkd = np.repeat(wk.reshape(D, 4, 1, 64), 2, axis=2).reshape(D, 512)
    wvd = np.repeat(wv.reshape(D, 4, 1, 64), 2, axis=2).reshape(D, 512)
    w_all = np.ascontiguousarray(np.concatenate([wq, wkd, wvd, wz], axis=1))
    sp = sink.reshape(4, 4, 2).transpose(0, 2, 1).reshape(32)
    sink_b = np.ascontiguousarray(np.broadcast_to(sp[None, :], (128, 32))).astype(np.float32)
    in_maps = []
    for k in range(NCORES):
        b, i = k // 4, k % 4
        lo, hi = i * TP - 128, (i + 1) * TP + 128
        xe = np.zeros((TP + 256, D), np.float32)
        cosw = np.zeros((128, TP + 256), np.float32)
        sinw = np.zeros((128, TP + 256), np.float32)
        slo, shi = max(lo, 0), min(hi, N)
        xe[slo - lo: shi - lo] = x[b, slo:shi]
        cosw[:, slo - lo: shi - lo] = atabs["cos"][:, slo:shi]
        sinw[:, slo - lo: shi - lo] = atabs["sin"][:, slo:shi]
        zero = np.zeros((128, 128), bf)
        mk = np.concatenate([atabs["maskP"], atabs["maskN"],
                             atabs["maskP"] if lo >= 0 else zero,
                             atabs["maskN"] if hi <= N else zero], axis=1)
        vecs = np.stack([g_l, sh[b], sc[b], sh[2], sc[2]]).astype(np.float32)
        in_maps.append({"xe": xe, "ctx": np.ascontiguousarray(ctx[b]), "vecs": vecs, "w_all": w_all,
                        "ident": tabs["ident"], "cos": cosw, "sin": sinw, "R": atabs["R"],
                        "masks": np.ascontiguousarray(mk), "sink": sink_b})
    ra = _launch(nca, in_maps)
    return run_generic_c([r["MT"] for r in ra], [r["szT"] for r in ra], x, gt, w_out, final_g)


def kernel(x, c, ctx, c_ctx, norm_g, ada_w, ada_b, four_w_in, four_w_out, attn_w_in, attn_sink, attn_w_out,
           conv_w_in, conv_dw_w, conv_dw_b, conv_ln_g, conv_ln_b, conv_w_out, final_g):
    f = lambda a: np.ascontiguousarray(np.asarray(a, dtype=np.float32))
    x, c, ctx, c_ctx, norm_g, ada_w, ada_b = map(f, (x, c, ctx, c_ctx, norm_g, ada_w, ada_b))
    four_w_in, four_w_out, attn_w_in, attn_sink, attn_w_out = map(f, (four_w_in, four_w_out, attn_w_in, attn_sink, attn_w_out))
    conv_w_in, conv_dw_w, conv_dw_b, conv_ln_g, conv_ln_b, conv_w_out, final_g = map(
        f, (conv_w_in, conv_dw_w, conv_dw_b, conv_ln_g, conv_ln_b, conv_w_out, final_g))
    tabs = const_tables()
    atabs = attn_tables(x.shape[1])
    mod = run_mod(c, c_ctx, ada_w, ada_b)
    x1, ctx1 = run_fourier_layer(x, ctx, mod[0], norm_g[0], four_w_in[0], four_w_out[0], tabs)
    x2 = run_attn_layer(x1, ctx1, mod[1], norm_g[1], attn_w_in[0], attn_sink[0], attn_w_out[0], tabs, atabs)
    x3 = run_conv_layer(x2, mod[2], norm_g[2], conv_w_in[0], conv_dw_w[0], conv_dw_b[0], conv_ln_g[0],
                        conv_ln_b[0], conv_w_out[0], tabs)
    out, _ = run_fourier_layer(x3, ctx1, mod[3], norm_g[3], four_w_in[1], four_w_out[1], tabs, final_g=final_g)
    return out.astype(np.float32)
```

```python
import numpy as np
import concourse.bass as bass
import concourse.mybir as mybir
from concourse.bass_utils import run_bass_kernel_spmd

F32 = mybir.dt.float32
BF16 = mybir.dt.bfloat16
AF = mybir.ActivationFunctionType
ALU = mybir.AluOpType
AX = mybir.AxisListType

NCORES = 8
D = 2048
KC = D // 128


class Tok:
    __slots__ = ("name", "writers", "readers", "sem", "cnt")

    def __init__(self, name):
        self.name = name
        self.writers = []
        self.readers = []
        self.sem = None
        self.cnt = 0


class Op:
    __slots__ = ("eng", "fn", "cdeps", "dwaits", "is_dma", "tok", "val", "signal", "inc")

    def __init__(self, eng, fn):
        self.eng = eng
        self.fn = fn
        self.cdeps = []
        self.dwaits = []
        self.is_dma = False
        self.tok = None
        self.val = 0
        self.signal = False
        self.inc = 16


class Prog:
    ENGS = ["sync", "scalar", "vector", "gpsimd", "tensor"]

    def __init__(self, nc):
        self.nc = nc
        self.q = {e: [] for e in self.ENGS}
        self.dma_toks = []
        self.nsem = 0

    def tok(self, name="t"):
        return Tok(name)

    def toks(self, name, n):
        return [Tok(f"{name}{i}") for i in range(n)]

    def _dep_on(self, op, prev):
        if prev.is_dma:
            op.dwaits.append((prev.tok, prev.tok.cnt))
        else:
            op.cdeps.append(prev)

    def _track(self, op, reads, writes, pwrites):
        for t in reads:
            for w in t.writers:
                self._dep_on(op, w)
        for t in writes:
            for w in t.writers:
                self._dep_on(op, w)
            for r in t.readers:
                self._dep_on(op, r)
        for t in pwrites:
            for r in t.readers:
                self._dep_on(op, r)
        for t in reads:
            t.readers.append(op)
        for t in writes:
            t.writers = [op]
            t.readers = []
        for t in pwrites:
            if t.readers:
                t.writers = [op]
                t.readers = []
            else:
                t.writers.append(op)

    def op(self, eng, fn, reads=(), writes=(), pwrites=()):
        o = Op(eng, fn)
        self._track(o, reads, writes, pwrites)
        self.q[eng].append(o)
        return o

    def dma(self, eng, out, in_, reads=(), writes=(), pwrites=(), semtok=None, **kw):
        if semtok is None:
            semtok = (list(writes) + list(pwrites) + list(reads))[0]
        o = Op(eng, lambda e: e.dma_start(out=out, in_=in_, **kw))
        o.is_dma = True
        o.tok = semtok
        self._track(o, reads, writes, pwrites)
        if semtok.sem is None:
            semtok.sem = self.nc.alloc_semaphore(f"d{self.nsem}_{semtok.name}")
            self.nsem += 1
            self.dma_toks.append(semtok)
        semtok.cnt += 16
        self.q[eng].append(o)
        return o

    def coll(self, o, reads=(), writes=(), pwrites=(), inc=16):
        semtok = (list(writes) + list(pwrites))[0]
        o.is_dma = True
        o.inc = inc
        o.tok = semtok
        self._track(o, reads, writes, pwrites)
        if semtok.sem is None:
            semtok.sem = self.nc.alloc_semaphore(f"d{self.nsem}_{semtok.name}")
            self.nsem += 1
            self.dma_toks.append(semtok)
        semtok.cnt += inc
        self.q[o.eng].append(o)
        return o

    def emit(self):
        nc = self.nc
        comp = ["scalar", "vector", "gpsimd", "tensor"]
        for e in self.ENGS:
            for o in self.q[e]:
                for d in o.cdeps:
                    d.signal = True
        esem = {}
        for e in comp:
            n = 0
            for o in self.q[e]:
                if o.signal and not o.is_dma:
                    n += 1
                    o.val = n
            if n:
                esem[e] = nc.alloc_semaphore(f"e_{e}")
                self.nsem += 1
        final_waits = [(t.sem, t.cnt) for t in self.dma_toks]
        q = self.q

        def run(e, eng):
            waited = {}
            for o in q[e]:
                ws = [(esem[d.eng], d.val) for d in o.cdeps] + [(t.sem, c) for t, c in o.dwaits]
                for s, v in ws:
                    if waited.get(id(s), 0) < v:
                        eng.wait_ge(s, v)
                        waited[id(s)] = v
                ins = o.fn(eng)
                if o.is_dma:
                    ins.then_inc(o.tok.sem, o.inc)
                elif o.signal:
                    ins.then_inc(esem[e], 1)
            if e == "sync":
                for s, v in final_waits:
                    if waited.get(id(s), 0) < v:
                        eng.wait_ge(s, v)

        with nc.Block() as block:
            @block.sync
            def _(eng):
                run("sync", eng)

            @block.scalar
            def _(eng):
                run("scalar", eng)

            @block.vector
            def _(eng):
                run("vector", eng)

            @block.gpsimd
            def _(eng):
                run("gpsimd", eng)

            @block.tensor
            def _(eng):
                run("tensor", eng)


def build_mod():
    nc = bass.Bass("TRN2", target_bir_lowering=False)
    cT = nc.dram_tensor("cT", [128, KC * 3], F32, kind="ExternalInput").ap()
    w = nc.dram_tensor("w", [4, D, 768], F32, kind="ExternalInput").ap()
    b = nc.dram_tensor("b", [4, 768], F32, kind="ExternalInput").ap()
    out = nc.dram_tensor("out", [4, 3, 768], F32, kind="ExternalOutput").ap()
    P = Prog(nc)
    c_sb = nc.alloc_sbuf_tensor("c_sb", [128, KC * 3], F32)
    sc_sb = nc.alloc_sbuf_tensor("sc_sb", [128, KC * 3], F32)
    w_sb = [nc.alloc_sbuf_tensor(f"w_sb{i}", [128, KC, 768], F32) for i in range(2)]
    b_sb = nc.alloc_sbuf_tensor("b_sb", [3, 4 * 768], F32)
    o_sb = nc.alloc_sbuf_tensor("o_sb", [3, 4 * 768], F32)
    ps = [nc.alloc_psum_tensor(f"ps{i}", [128, 512], F32) for i in range(2)]
    t_c, t_sc, t_b, t_o = P.tok("c"), P.tok("sc"), P.tok("b"), P.tok("o")
    t_w = P.toks("w", 2)
    t_ps = P.toks("ps", 2)

    P.dma("sync", c_sb[:, :], cT, writes=[t_c])
    for l in range(4):
        P.dma("sync", b_sb[:, l * 768:(l + 1) * 768], b[l:l + 1, :].broadcast_to([3, 768]), pwrites=[t_b])
    P.op("scalar", lambda e: e.activation(out=sc_sb[:, :], in_=c_sb[:, :], func=AF.Silu),
         reads=[t_c], writes=[t_sc])
    n = 0
    for l in range(4):
        wb = l % 2
        P.dma("sync", w_sb[wb][:, :, :], w[l].rearrange("(kc p) n -> p kc n", p=128), writes=[t_w[wb]])
        for h in range(2):
            pb = n % 2
            n += 1

            def mm(e, wb=wb, h=h, pb=pb):
                ins = None
                for kc in range(KC):
                    ins = e.matmul(ps[pb][0:3, 0:384], lhsT=sc_sb[:, kc * 3:(kc + 1) * 3],
                                   rhs=w_sb[wb][:, kc, h * 384:(h + 1) * 384],
                                   start=(kc == 0), stop=(kc == KC - 1))
                return ins
            P.op("tensor", mm, reads=[t_sc, t_w[wb]], writes=[t_ps[pb]])
            sl = slice(l * 768 + h * 384, l * 768 + (h + 1) * 384)
            P.op("vector", lambda e, pb=pb, sl=sl: e.tensor_tensor(
                out=o_sb[:, sl], in0=ps[pb][0:3, 0:384], in1=b_sb[:, sl], op=ALU.add),
                reads=[t_ps[pb], t_b], pwrites=[t_o])
    P.dma("sync", out.rearrange("l r n -> r l n"), o_sb[:, :].rearrange("r (l n) -> r l n", l=4),
          reads=[t_o], writes=[P.tok("out")])
    P.emit()
    return nc


def run_mod(c, c_ctx, ada_w, ada_b):
    cc = np.concatenate([c, c_ctx[None, :]], axis=0)
    cT = np.ascontiguousarray(cc.reshape(3, KC, 128).transpose(2, 1, 0)).reshape(128, KC * 3)
    nc = build_mod()
    in_maps = []
    for k in range(NCORES):
        in_maps.append({
            "cT": cT,
            "w": np.ascontiguousarray(ada_w[:, :, 768 * k:768 * (k + 1)]),
            "b": np.ascontiguousarray(ada_b[:, 768 * k:768 * (k + 1)]),
        })
    res = run_bass_kernel_spmd(nc, in_maps, core_ids=list(range(NCORES)))
    return np.concatenate([r["out"] for r in res.results], axis=2)


BF = None


def _bf16():
    import ml_dtypes
    return ml_dtypes.bfloat16


def const_tables():
    bf = _bf16()
    t = {}
    t["ident"] = np.eye(128, dtype=np.float32).astype(bf)
    c = np.arange(256)[:, None].astype(np.float64)
    cp = np.arange(256)[None, :].astype(np.float64)
    ang = 2 * np.pi * c * cp / 256.0
    C, S = np.cos(ang) / 16.0, np.sin(ang) / 16.0
    fc = np.zeros((256, 2, 2, 128))
    for half in range(2):
        fc[:, half, 0, :] = C[:, half * 128:(half + 1) * 128]
        fc[:, half, 1, :] = -S[:, half * 128:(half + 1) * 128]
    t["fc"] = fc.reshape(256, 512).astype(np.float32).astype(bf)
    t["pc"] = np.concatenate([C, S], axis=1).astype(np.float32).astype(bf)
    n = np.arange(128)[:, None].astype(np.float64)
    k = np.arange(128)[None, :].astype(np.float64)
    a = 2 * np.pi * n * k / 128.0
    C1, S1 = np.cos(a), np.sin(a)
    t["t1a"] = (np.concatenate([C1, -S1], axis=1) / 8.0).astype(np.float32).astype(bf)
    t["t1b"] = (np.concatenate([S1, C1], axis=1) / 8.0).astype(np.float32).astype(bf)
    t["t3"] = (np.concatenate([C1, S1], axis=1) / 16.0).astype(np.float32).astype(bf)
    tw = 2 * np.pi * n * k / 16384.0
    Tr, Ti = np.cos(tw), -np.sin(tw)
    t["tw1"] = np.concatenate([Tr, Ti], axis=1).astype(np.float32)
    t["tw2"] = np.concatenate([Ti, Tr], axis=1).astype(np.float32)
    return t


class Ctx:
    def __init__(self, nc, P):
        self.nc, self.P = nc, P
        self.n = 0

    def sb(self, shape, dt, name=None):
        self.n += 1
        return self.nc.alloc_sbuf_tensor((name or "sb") + f"_s{self.n}", list(shape), dt)

    def ps(self, shape, dt, name=None):
        self.n += 1
        return self.nc.alloc_psum_tensor((name or "ps") + f"_p{self.n}", list(shape), dt)

    def din(self, name, shape, dt):
        return self.nc.dram_tensor(name, list(shape), dt, kind="ExternalInput").ap()

    def dout(self, name, shape, dt):
        return self.nc.dram_tensor(name, list(shape), dt, kind="ExternalOutput").ap()


class Rot:
    def __init__(self, bufs, toks):
        self.bufs, self.toks, self.i = bufs, toks, 0

    def next(self):
        j = self.i % len(self.bufs)
        self.i += 1
        return self.bufs[j], self.toks[j]


def mk_rot(cx, n, shape, dt, name, psum=False):
    bufs = [(cx.ps if psum else cx.sb)(shape, dt, f"{name}{i}") for i in range(n)]
    return Rot(bufs, cx.P.toks(name, n))


def emit_rstd(cx, R, ss, t_ss, eps, n=D):
    P = cx.P
    rs, t_rs = R["rs"].next()
    t_mid = P.tok("rsmid")
    P.op("vector", lambda e: e.tensor_scalar(out=rs[:, 0:1], in0=ss[:, 0:1], scalar1=1.0 / n, scalar2=eps,
                                             op0=ALU.mult, op1=ALU.add), reads=[t_ss], writes=[t_rs])
    P.op("scalar", lambda e: e.activation(out=rs[:, 0:1], in_=rs[:, 0:1], func=AF.Sqrt),
         reads=[t_rs], writes=[t_rs])
    P.op("vector", lambda e: e.reciprocal(out=rs[:, 1:2], in_=rs[:, 0:1]), reads=[t_rs], writes=[t_rs])
    return rs, t_rs


def emit_norm_transpose(cx, R, src_rows, ntiles, A_bc, B_bc, t_AB, hT, t_hT, eps=1e-6):
    P = cx.P
    for i in range(ntiles):
        xt, t_x = R["xt"].next()
        P.dma("sync", xt[:, :], src_rows(i), writes=[t_x])
        sq, t_sq = R["sq"].next()
        ss, t_ss = R["ss"].next()
        P.op("scalar", lambda e, xt=xt, sq=sq, ss=ss: e.activation(
            out=sq[:, :], in_=xt[:, :], func=AF.Square, accum_out=ss[:, 0:1]),
            reads=[t_x], writes=[t_sq, t_ss])
        rs, t_rs = emit_rstd(cx, R, ss, t_ss, eps)
        tt, t_tt = R["tt"].next()
        P.op("vector", lambda e, xt=xt, rs=rs, tt=tt: e.scalar_tensor_tensor(
            out=tt[:, :], in0=xt[:, :], scalar=rs[:, 1:2], in1=A_bc[:, :], op0=ALU.mult, op1=ALU.mult),
            reads=[t_x, t_rs, t_AB], writes=[t_tt])
        hb, t_hb = R["hb"].next()
        P.op("vector", lambda e, tt=tt, hb=hb: e.tensor_tensor(
            out=hb[:, :], in0=tt[:, :], in1=B_bc[:, :], op=ALU.add),
            reads=[t_tt, t_AB], writes=[t_hb])
        for half in range(2):
            pT, t_pT = R["pT"].next()

            def tr(e, hb=hb, pT=pT, half=half):
                ins = None
                for j in range(8):
                    kc = half * 8 + j
                    ins = e.transpose(pT[:, j * 128:(j + 1) * 128], hb[:, kc * 128:(kc + 1) * 128],
                                      R["ident"][:, :])
                return ins
            P.op("tensor", tr, reads=[t_hb, R["t_ident"]], writes=[t_pT])
            eng = "scalar" if half == 0 else "vector"

            def ev(e, pT=pT, half=half, i=i, eng=eng):
                o = hT[:, half * 8:(half + 1) * 8, i * 128:(i + 1) * 128]
                s = pT[:, :].rearrange("p (j t) -> p j t", j=8)
                if eng == "scalar":
                    return e.copy(out=o, in_=s)
                return e.tensor_copy(out=o, in_=s)
            P.op(eng, ev, reads=[t_pT], pwrites=[t_hT])


def load_bc(cx, dst, t_dst, row_ap, eng="sync"):
    cx.P.dma(eng, dst[:, :], row_ap.broadcast_to([128, D]), pwrites=[t_dst])


ST = 512


def std_rots(cx, acc_n=3, with_pT=True):
    R = {}
    R["xt"] = mk_rot(cx, 2, [128, D], F32, "xt")
    R["sq"] = mk_rot(cx, 1, [128, D], BF16, "sq")
    R["ss"] = mk_rot(cx, 2, [128, 1], F32, "ss")
    R["rs"] = mk_rot(cx, 2, [128, 2], F32, "rs")
    R["tt"] = mk_rot(cx, 2, [128, D], F32, "tt")
    R["hb"] = mk_rot(cx, 2, [128, D], BF16, "hb")
    if with_pT:
        R["pT"] = mk_rot(cx, 2, [128, 1024], BF16, "pT", psum=True)
    if acc_n:
        R["acc"] = mk_rot(cx, acc_n, [128, 512], F32, "acc", psum=True)
    R["wsl"] = mk_rot(cx, 2, [128, KC, 512], BF16, "wsl")
    return R


def load_ident(cx, R, ident_d):
    R["ident"] = cx.sb([128, 128], BF16, "ident")
    R["t_ident"] = cx.P.tok("ident")
    cx.P.dma("sync", R["ident"][:, :], ident_d, writes=[R["t_ident"]])


def load_slab(cx, R, w_d, c0, ncols=512):
    wsl, t_w = R["wsl"].next()
    cx.P.dma("gpsimd", wsl[:, :, 0:ncols], w_d[:, c0:c0 + ncols].rearrange("(kc p) n -> p kc n", p=128),
             writes=[t_w])
    return wsl, t_w


def build_four_a(T=4096, NCTX=256):
    nc = bass.Bass("TRN2", target_bir_lowering=False)
    P = Prog(nc)
    cx = Ctx(nc, P)
    x_d = cx.din("x", [T, D], F32)
    ctx_d = cx.din("ctx", [NCTX, D], F32)
    vecs = cx.din("vecs", [6, D], F32)
    w_in = cx.din("w_in", [D, 2 * D], F32)
    w_out = cx.din("w_out", [D, D], F32)
    ident_d = cx.din("ident", [128, 128], BF16)
    fc_d = cx.din("fc", [256, 512], BF16)
    pc_d = cx.din("pc", [256, 512], BF16)
    V_d = cx.dout("V", [T, 8 * 512], BF16)
    szT_d = cx.dout("szT", [D, T], F32)
    ctx1_d = cx.dout("ctx1", [NCTX, D], F32)

    R = std_rots(cx)
    load_ident(cx, R, ident_d)
    wb, t_wb = precast_weight(cx, w_in, 2 * D, "w_in_bf")
    wob, t_wob = precast_weight(cx, w_out, D, "w_out_bf")
    fc = cx.sb([128, 2, 512], BF16, "fc")
    pc = cx.sb([128, 2, 512], BF16, "pc")
    t_tab = P.tok("tab")
    P.dma("sync", fc[:, :, :], fc_d.rearrange("(cc p) n -> p cc n", p=128), pwrites=[t_tab])
    P.dma("sync", pc[:, :, :], pc_d.rearrange("(cc p) n -> p cc n", p=128), pwrites=[t_tab])
    A_bc = cx.sb([128, D], F32, "A_bc")
    B_bc = cx.sb([128, D], F32, "B_bc")
    G_bc = cx.sb([128, D], F32, "G_bc")
    t_AB, t_G = P.tok("AB"), P.tok("G")
    hT = cx.sb([128, KC, ST], BF16, "hT")
    uT = cx.sb([128, KC, ST], BF16, "uT")
    t_hT, t_uT = P.tok("hT"), P.tok("uT")
    szst = mk_rot(cx, 3, [128, 512], F32, "szst")
    Vt = mk_rot(cx, 2, [128, 8 * 512], BF16, "Vt")
    pv = mk_rot(cx, 2, [128, 512], F32, "pv", psum=True)
    szc = cx.sb([128, KC, NCTX], F32, "szc")
    Vc = cx.sb([128, 2, 8 * 512], BF16, "Vc")
    t_szc, t_Vc = P.tok("szc"), P.tok("Vc")

    def set_AB(scale_row, shift_row):
        g_bc, t_g = R["tt"].next()
        s_bc, t_s = R["tt"].next()
        P.dma("sync", g_bc[:, :], vecs[0:1, :].broadcast_to([128, D]), writes=[t_g])
        P.dma("sync", s_bc[:, :], vecs[scale_row:scale_row + 1, :].broadcast_to([128, D]), writes=[t_s])
        P.dma("sync", B_bc[:, :], vecs[shift_row:shift_row + 1, :].broadcast_to([128, D]), pwrites=[t_AB])
        P.op("vector", lambda e: e.scalar_tensor_tensor(
            out=A_bc[:, :], in0=s_bc[:, :], scalar=1.0, in1=g_bc[:, :], op0=ALU.add, op1=ALU.mult),
            reads=[t_g, t_s], pwrites=[t_AB])

    def project(ntok, ev_fn):
        for s in range(8):
            wsl, t_w = load_slab_bf(cx, R, wb, t_wb, s * 512)
            for j in range(4):
                acc, t_acc = R["acc"].next()

                def mm(e, wsl=wsl, j=j, acc=acc):
                    ins = None
                    for kc in range(KC):
                        ins = e.matmul(acc[:, 0:ntok], lhsT=wsl[:, kc, j * 128:(j + 1) * 128],
                                       rhs=hT[:, kc, 0:ntok], start=(kc == 0), stop=(kc == KC - 1))
                    return ins
                P.op("tensor", mm, reads=[t_w, t_hT], writes=[t_acc])
                ev_fn(4 * s + j, acc, t_acc)

    def chan_dft(i, dst_ap, dst_tok, g):
        pvb, t_pv = pv.next()

        def mm(e):
            ins = None
            for cc in range(2):
                ins = e.matmul(pvb[:, :], lhsT=uT[:, 2 * g + cc, i * 128:(i + 1) * 128],
                               rhs=fc[:, cc, :], start=(cc == 0), stop=(cc == 1))
            return ins
        P.op("tensor", mm, reads=[t_uT, t_tab], writes=[t_pv])
        if g % 2 == 0:
            P.op("scalar", lambda e: e.copy(out=dst_ap, in_=pvb[:, :]), reads=[t_pv], pwrites=[dst_tok])
        else:
            P.op("vector", lambda e: e.tensor_copy(out=dst_ap, in_=pvb[:, :]), reads=[t_pv], pwrites=[dst_tok])

    set_AB(4, 3)
    P.dma("sync", G_bc[:, :], vecs[5:6, :].broadcast_to([128, D]), writes=[t_G])
    emit_norm_transpose(cx, R, lambda i: ctx_d[i * 128:(i + 1) * 128, :], NCTX // 128, A_bc, B_bc, t_AB, hT, t_hT)

    def ev_ctx(ci, acc, t_acc):
        if ci < KC:
            P.op("vector", lambda e: e.tensor_copy(out=uT[:, ci, 0:NCTX], in_=acc[:, 0:NCTX]),
                 reads=[t_acc], pwrites=[t_uT])
        else:
            P.op("scalar", lambda e: e.activation(out=szc[:, ci - KC, :], in_=acc[:, 0:NCTX], func=AF.Silu),
                 reads=[t_acc], pwrites=[t_szc])
    project(NCTX, ev_ctx)
    for i in range(NCTX // 128):
        for g in range(8):
            chan_dft(i, Vc[:, i, g * 512:(g + 1) * 512], t_Vc, g)
    for q in range(KC):
        acc, t_acc = R["acc"].next()

        def mm(e, q=q, acc=acc):
            ins = None
            n = 0
            for i in range(2):
                for ri in range(2):
                    ins = e.matmul(acc[:, 0:NCTX], lhsT=Vc[:, i, q * 256 + ri * 128: q * 256 + (ri + 1) * 128],
                                   rhs=pc[:, i, ri * 256:(ri + 1) * 256], start=(n == 0), stop=(n == 3))
                    n += 1
            return ins
        P.op("tensor", mm, reads=[t_Vc, t_tab], writes=[t_acc])
        P.op("vector", lambda e, q=q, acc=acc: e.tensor_tensor(
            out=hT[:, q, 0:NCTX], in0=acc[:, 0:NCTX], in1=szc[:, q, :], op=ALU.mult),
            reads=[t_acc, t_szc], pwrites=[t_hT])
    cts = []
    for i in range(NCTX // 128):
        xt, t_x = R["xt"].next()
        P.dma("sync", xt[:, :], ctx_d[i * 128:(i + 1) * 128, :], writes=[t_x])
        ot, t_o = R["tt"].next()
        cts.append((xt, t_x, ot, t_o))
    for s in range(4):
        wsl, t_w = load_slab_bf(cx, R, wob, t_wob, s * 512)
        for i in range(NCTX // 128):
            xt, t_x, ot, t_o = cts[i]
            acc, t_acc = R["acc"].next()

            def mm(e, wsl=wsl, i=i, acc=acc):
                ins = None
                for kc in range(KC):
                    ins = e.matmul(acc[:, :], lhsT=hT[:, kc, i * 128:(i + 1) * 128], rhs=wsl[:, kc, :],
                                   start=(kc == 0), stop=(kc == KC - 1))
                return ins
            P.op("tensor", mm, reads=[t_w, t_hT], writes=[t_acc])
            sl = slice(s * 512, (s + 1) * 512)

            def ep(e, acc=acc, ot=ot, xt=xt, sl=sl):
                e.tensor_tensor(out=ot[:, sl], in0=acc[:, :], in1=G_bc[:, sl], op=ALU.mult)
                return e.tensor_tensor(out=ot[:, sl], in0=ot[:, sl], in1=xt[:, sl], op=ALU.add)
            P.op("vector", ep, reads=[t_acc, t_G, t_x], pwrites=[t_o])
    for i in range(NCTX // 128):
        xt, t_x, ot, t_o = cts[i]
        P.dma("sync", ctx1_d[i * 128:(i + 1) * 128, :], ot[:, :], reads=[t_o], writes=[P.tok("ctx1")])

    set_AB(2, 1)
    for st in range(T // ST):
        emit_norm_transpose(cx, R, lambda i, st=st: x_d[st * ST + i * 128: st * ST + (i + 1) * 128, :],
                            ST // 128, A_bc, B_bc, t_AB, hT, t_hT)

        def ev_lat(ci, acc, t_acc, st=st):
            if ci < KC:
                P.op("vector", lambda e: e.tensor_copy(out=uT[:, ci, :], in_=acc[:, :]),
                     reads=[t_acc], pwrites=[t_uT])
            else:
                sb, t_sb = szst.next()
                P.op("scalar", lambda e: e.activation(out=sb[:, :], in_=acc[:, :], func=AF.Silu),
                     reads=[t_acc], writes=[t_sb])
                c0 = (ci - KC) * 128
                P.dma("sync", szT_d[c0:c0 + 128, st * ST:(st + 1) * ST], sb[:, :], reads=[t_sb],
                      writes=[P.tok("szT")], semtok=t_sb)
        project(ST, ev_lat)
        for i in range(ST // 128):
            vb, t_vb = Vt.next()
            for g in range(8):
                chan_dft(i, vb[:, g * 512:(g + 1) * 512], t_vb, g)
            r0 = st * ST + i * 128
            P.dma("sync", V_d[r0:r0 + 128, :], vb[:, :], reads=[t_vb], writes=[P.tok("V")], semtok=t_vb)
    P.emit()
    return nc


def build_four_b(NSB=4):
    nc = bass.Bass("TRN2", target_bir_lowering=False)
    P = Prog(nc)
    cx = Ctx(nc, P)
    V_d = cx.din("V2", [16384, NSB * 256], BF16)
    t1a_d = cx.din("t1a", [128, 256], BF16)
    t1b_d = cx.din("t1b", [128, 256], BF16)
    t3_d = cx.din("t3", [128, 256], BF16)
    tw1_d = cx.din("tw1", [128, 256], F32)
    tw2_d = cx.din("tw2", [128, 256], F32)
    MT_d = cx.dout("MT", [NSB * 128, 16384], F32)

    t1a = cx.sb([128, 256], BF16, "t1a")
    t1b = cx.sb([128, 256], BF16, "t1b")
    t3 = cx.sb([128, 256], BF16, "t3")
    tw1 = cx.sb([128, 256], F32, "tw1")
    tw2 = cx.sb([128, 256], F32, "tw2")
    t_tab = P.tok("tab")
    for dst, src in ((t1a, t1a_d), (t1b, t1b_d), (t3, t3_d), (tw1, tw1_d), (tw2, tw2_d)):
        P.dma("sync", dst[:, :], src, pwrites=[t_tab])
    XM = cx.sb([128, 16384], F32, "XM")
    t_XM = P.tok("XM")
    X = XM[:, :].bitcast(BF16).rearrange("p (n r c) -> p n r c", n=128, r=2, c=128)
    M = XM[:, :].rearrange("p (k2 k1) -> p k2 k1", k2=128, k1=128)
    Bt = cx.sb([128, 2, 128, 128], BF16, "Bt")
    t_B = P.tok("B")
    p1r = mk_rot(cx, 2, [128, 2, 256], F32, "p1", psum=True)
    p3r = mk_rot(cx, 2, [128, 4, 128], F32, "p3", psum=True)
    P1r = mk_rot(cx, 2, [128, 2, 256], F32, "P1")
    P2r = mk_rot(cx, 2, [128, 2, 256], F32, "P2")
    Vv = V_d.rearrange("(n1 n2) c -> n1 n2 c", n1=128, n2=128)

    for sb in range(NSB):
        for q in range(8):
            P.dma("sync", X[:, q * 16:(q + 1) * 16, :, :],
                  Vv[:, q * 16:(q + 1) * 16, sb * 256:(sb + 1) * 256].rearrange("p n (r c) -> p n r c", r=2),
                  pwrites=[t_XM])
        for cp in range(64):
            p1, t_p1 = p1r.next()

            def mm1(e, cp=cp, p1=p1):
                ins = None
                for cj in range(2):
                    c = 2 * cp + cj
                    e.matmul(p1[:, cj, :], lhsT=X[:, :, 0, c], rhs=t1a[:, :], start=True, stop=False)
                    ins = e.matmul(p1[:, cj, :], lhsT=X[:, :, 1, c], rhs=t1b[:, :], start=False, stop=True)
                return ins
            P.op("tensor", mm1, reads=[t_XM, t_tab], writes=[t_p1])
            P1, t_P1 = P1r.next()
            P2, t_P2 = P2r.next()
            P.op("vector", lambda e, p1=p1, P1=P1: e.tensor_tensor(
                out=P1[:, :, :], in0=p1[:, :, :], in1=tw1[:, :].unsqueeze(1).broadcast_to([128, 2, 256]),
                op=ALU.mult), reads=[t_p1, t_tab], writes=[t_P1])
            P.op("vector", lambda e, p1=p1, P2=P2: e.tensor_tensor(
                out=P2[:, :, :], in0=p1[:, :, :], in1=tw2[:, :].unsqueeze(1).broadcast_to([128, 2, 256]),
                op=ALU.mult), reads=[t_p1, t_tab], writes=[t_P2])
            P.op("gpsimd", lambda e, cp=cp, P1=P1: e.tensor_tensor(
                out=Bt[:, 0, :, 2 * cp:2 * cp + 2], in0=P1[:, :, 0:128].rearrange("p c k -> p k c"),
                in1=P1[:, :, 128:256].rearrange("p c k -> p k c"), op=ALU.subtract),
                reads=[t_P1], pwrites=[t_B])
            P.op("gpsimd", lambda e, cp=cp, P2=P2: e.tensor_tensor(
                out=Bt[:, 1, :, 2 * cp:2 * cp + 2], in0=P2[:, :, 0:128].rearrange("p c k -> p k c"),
                in1=P2[:, :, 128:256].rearrange("p c k -> p k c"), op=ALU.add),
                reads=[t_P2], pwrites=[t_B])
        for kg in range(32):
            p3, t_p3 = p3r.next()

            def mm3(e, kg=kg, p3=p3):
                ins = None
                for kk in range(4):
                    k1 = 4 * kg + kk
                    e.matmul(p3[:, kk, :], lhsT=Bt[:, 0, k1, :], rhs=t3[:, 0:128], start=True, stop=False)
                    ins = e.matmul(p3[:, kk, :], lhsT=Bt[:, 1, k1, :], rhs=t3[:, 128:256], start=False, stop=True)
                return ins
            P.op("tensor", mm3, reads=[t_B, t_tab], writes=[t_p3])
            o = M[:, :, 4 * kg:4 * kg + 4]
            s = p3[:, :, :].rearrange("p a k -> p k a")
            if kg % 2 == 0:
                P.op("scalar", lambda e, o=o, s=s: e.copy(out=o, in_=s), reads=[t_p3], pwrites=[t_XM])
            else:
                P.op("vector", lambda e, o=o, s=s: e.tensor_copy(out=o, in_=s), reads=[t_p3], pwrites=[t_XM])
        for q in range(4):
            P.dma("sync", MT_d[sb * 128:(sb + 1) * 128, q * 4096:(q + 1) * 4096], XM[:, q * 4096:(q + 1) * 4096],
                  reads=[t_XM], writes=[P.tok("MT")], semtok=t_XM)
    P.emit()
    return nc


STC = 256


def load_wout_resident(cx, w_out_d):
    P = cx.P
    Wo = cx.sb([128, KC, D], BF16, "Wo")
    t_Wo = P.tok("Wo")
    for s in range(4):
        P.dma("gpsimd", Wo[:, :, s * 512:(s + 1) * 512],
              w_out_d[:, s * 512:(s + 1) * 512].rearrange("(kc p) n -> p kc n", p=128), pwrites=[t_Wo])
    return Wo, t_Wo


def emit_out_tile(cx, R, yT, t_yT, tcol, Wo, t_Wo, G_bc, t_G, x_rows, out_rows, final, FG_bc, t_FG, eps=1e-6):
    P = cx.P
    xt, t_x = R["xt"].next()
    P.dma("sync", xt[:, :], x_rows, writes=[t_x])
    ot, t_o = R["tt"].next()
    for s in range(4):
        acc, t_acc = R["acc"].next()

        def mm(e, s=s, acc=acc):
            ins = None
            for kc in range(KC):
                ins = e.matmul(acc[:, :], lhsT=yT[:, kc, tcol:tcol + 128], rhs=Wo[:, kc, s * 512:(s + 1) * 512],
                               start=(kc == 0), stop=(kc == KC - 1))
            return ins
        P.op("tensor", mm, reads=[t_Wo, t_yT], writes=[t_acc])
        sl = slice(s * 512, (s + 1) * 512)

        def ep(e, acc=acc, sl=sl):
            e.tensor_tensor(out=ot[:, sl], in0=acc[:, :], in1=G_bc[:, sl], op=ALU.mult)
            return e.tensor_tensor(out=ot[:, sl], in0=ot[:, sl], in1=xt[:, sl], op=ALU.add)
        P.op("vector", ep, reads=[t_acc, t_G, t_x], pwrites=[t_o])
    if final:
        sq, t_sq = R["sq"].next()
        ss, t_ss = R["ss"].next()
        P.op("scalar", lambda e: e.activation(out=sq[:, :], in_=ot[:, :], func=AF.Square, accum_out=ss[:, 0:1]),
             reads=[t_o], writes=[t_sq, t_ss])
        rs, t_rs = emit_rstd(cx, R, ss, t_ss, eps)
        P.op("vector", lambda e: e.scalar_tensor_tensor(
            out=ot[:, :], in0=ot[:, :], scalar=rs[:, 1:2], in1=FG_bc[:, :], op0=ALU.mult, op1=ALU.mult),
            reads=[t_rs, t_FG], writes=[t_o])
    P.dma("sync", out_rows, ot[:, :], reads=[t_o], writes=[P.tok("xo")], semtok=t_o)


def build_four_c(T=4096, final=False):
    nc = bass.Bass("TRN2", target_bir_lowering=False)
    P = Prog(nc)
    cx = Ctx(nc, P)
    MT_d = cx.din("MT", [D, T], F32)
    szT_d = cx.din("szT", [D, T], F32)
    x_d = cx.din("x", [T, D], F32)
    vecs = cx.din("vecs", [2, D], F32)
    w_out = cx.din("w_out", [D, D], F32)
    xo_d = cx.dout("xo", [T, D], F32)
    R = {}
    R["xt"] = mk_rot(cx, 2, [128, D], F32, "xt")
    R["tt"] = mk_rot(cx, 2, [128, D], F32, "tt")
    R["acc"] = mk_rot(cx, 4, [128, 512], F32, "acc", psum=True)
    R["sq"] = mk_rot(cx, 1, [128, D], BF16, "sq")
    R["ss"] = mk_rot(cx, 2, [128, 1], F32, "ss")
    R["rs"] = mk_rot(cx, 2, [128, 2], F32, "rs")
    G_bc = cx.sb([128, D], F32, "G_bc")
    FG_bc = cx.sb([128, D], F32, "FG_bc")
    t_G, t_FG = P.tok("G"), P.tok("FG")
    P.dma("sync", G_bc[:, :], vecs[0:1, :].broadcast_to([128, D]), writes=[t_G])
    P.dma("sync", FG_bc[:, :], vecs[1:2, :].broadcast_to([128, D]), writes=[t_FG])
    Wo, t_Wo = load_wout_resident(cx, w_out)
    Ms = mk_rot(cx, 1, [128, KC, STC], F32, "Ms")
    Zs = mk_rot(cx, 1, [128, KC, STC], F32, "Zs")
    yTr = mk_rot(cx, 2, [128, KC, STC], BF16, "yT")
    for st in range(T // STC):
        ms, t_ms = Ms.next()
        zs, t_zs = Zs.next()
        tsl = slice(st * STC, (st + 1) * STC)
        P.dma("sync", ms[:, :, :], MT_d[:, tsl].rearrange("(kc p) t -> p kc t", p=128), writes=[t_ms])
        P.dma("sync", zs[:, :, :], szT_d[:, tsl].rearrange("(kc p) t -> p kc t", p=128), writes=[t_zs])
        yT, t_yT = yTr.next()
        P.op("vector", lambda e, ms=ms, zs=zs, yT=yT: e.tensor_tensor(
            out=yT[:, 0:8, :], in0=ms[:, 0:8, :], in1=zs[:, 0:8, :], op=ALU.mult),
            reads=[t_ms, t_zs], pwrites=[t_yT])
        P.op("gpsimd", lambda e, ms=ms, zs=zs, yT=yT: e.tensor_tensor(
            out=yT[:, 8:16, :], in0=ms[:, 8:16, :], in1=zs[:, 8:16, :], op=ALU.mult),
            reads=[t_ms, t_zs], pwrites=[t_yT])
        for i in range(STC // 128):
            r0 = st * STC + i * 128
            emit_out_tile(cx, R, yT, t_yT, i * 128, Wo, t_Wo, G_bc, t_G, x_d[r0:r0 + 128, :],
                          xo_d[r0:r0 + 128, :], final, FG_bc, t_FG)
    P.emit()
    return nc


_NC_CACHE = {}


def _get_nc(name, fn, *a, **kw):
    key = (name, a, tuple(sorted(kw.items())))
    if key not in _NC_CACHE:
        _NC_CACHE[key] = fn(*a, **kw)
    return _NC_CACHE[key]


_TRACE = [False]


def _launch(nc, in_maps):
    if _TRACE[0]:
        res = run_bass_kernel_spmd(nc, in_maps, core_ids=list(range(NCORES)), trace=True)
        print("[launch] exec_time_ns", res.exec_time_ns, "profile", res.profile_json, flush=True)
    else:
        res = run_bass_kernel_spmd(nc, in_maps, core_ids=list(range(NCORES)))
    return res.results


def run_fourier_layer(x, ctx, mod_l, g_l, w_in, w_out, tabs, final_g=None, with_ctx=True):
    B, N, _ = x.shape
    TP = N // 4
    sh, sc, gt = mod_l[:, 0:D], mod_l[:, D:2 * D], mod_l[:, 2 * D:3 * D]
    nca = _get_nc("four_a", build_four_a)
    in_maps = []
    for k in range(NCORES):
        b, i = k // 4, k % 4
        vecs = np.stack([g_l, sh[b], sc[b], sh[2], sc[2], gt[2]]).astype(np.float32)
        in_maps.append({"x": np.ascontiguousarray(x[b, i * TP:(i + 1) * TP]), "ctx": np.ascontiguousarray(ctx[b]),
                        "vecs": vecs, "w_in": w_in, "w_out": w_out, "ident": tabs["ident"],
                        "fc": tabs["fc"], "pc": tabs["pc"]})
    ra = _launch(nca, in_maps)
    ctx1 = np.stack([ra[0]["ctx1"], ra[4]["ctx1"]])
    ncb = _get_nc("four_b", build_four_b)
    in_maps = []
    for k in range(NCORES):
        b, j = k // 4, k % 4
        V2 = np.concatenate([ra[b * 4 + i]["V"][:, j * 1024:(j + 1) * 1024] for i in range(4)], axis=0)
        in_maps.append({"V2": np.ascontiguousarray(V2), "t1a": tabs["t1a"], "t1b": tabs["t1b"], "t3": tabs["t3"],
                        "tw1": tabs["tw1"], "tw2": tabs["tw2"]})
    rb = _launch(ncb, in_maps)
    ncc = _get_nc("four_c", build_four_c, final=final_g is not None)
    in_maps = []
    for k in range(NCORES):
        b, i = k // 4, k % 4
        MT = np.concatenate([rb[b * 4 + j]["MT"][:, i * TP:(i + 1) * TP] for j in range(4)], axis=0)
        vecs = np.stack([gt[b], final_g if final_g is not None else gt[b]]).astype(np.float32)
        in_maps.append({"MT": np.ascontiguousarray(MT), "szT": ra[k]["szT"],
                        "x": np.ascontiguousarray(x[b, i * TP:(i + 1) * TP]), "vecs": vecs, "w_out": w_out})
    rc = _launch(ncc, in_maps)
    xo = np.stack([np.concatenate([rc[b * 4 + i]["xo"] for i in range(4)], axis=0) for b in range(B)])
    return xo, ctx1


HALO = 15
CW = 31


def precast_weight(cx, w_d, ncols, name):
    P = cx.P
    wb = cx.nc.dram_tensor(name, [D, ncols], BF16).ap()
    t_wb = P.tok(name)
    step = 512
    for r in range(0, D, step):
        P.dma("gpsimd", wb[r:r + step, :], w_d[r:r + step, :], pwrites=[t_wb])
    return wb, t_wb


def load_slab_bf(cx, R, wb, t_wb, c0, ncols=512):
    wsl, t_w = R["wsl"].next()
    cx.P.dma("sync", wsl[:, :, 0:ncols], wb[:, c0:c0 + ncols].rearrange("(kc p) n -> p kc n", p=128),
             reads=[t_wb], writes=[t_w])
    return wsl, t_w


def build_conv_a(T=4096):
    nc = bass.Bass("TRN2", target_bir_lowering=False)
    P = Prog(nc)
    cx = Ctx(nc, P)
    TE = T + 2 * HALO
    NW = ST + 2 * HALO
    x_d = cx.din("xe", [TE + 128, D], F32)
    vecs = cx.din("vecs", [3, D], F32)
    pv_d = cx.din("pvec", [128, KC * (CW + 3)], F32)
    hm_d = cx.din("hm", [128, 2], F32)
    w_in = cx.din("w_in", [D, 3 * D], F32)
    ident_d = cx.din("ident", [128, 128], BF16)
    MT_d = cx.dout("MT", [D, T], F32)
    szT_d = cx.dout("szT", [D, T], F32)

    R = std_rots(cx, acc_n=0)
    R["acc"] = mk_rot(cx, 2, [128, 512], F32, "accm", psum=True)
    accx = cx.ps([128, 512], F32, "accx")
    t_accx = P.toks("accx", 2)
    yacc = cx.ps([128, 512], F32, "yacc")
    t_yacc = P.tok("yacc")
    load_ident(cx, R, ident_d)
    identf = cx.sb([128, 128], F32, "identf")
    dgr = mk_rot(cx, 2, [128, CW, 128], BF16, "dg")
    wb, t_wb = precast_weight(cx, w_in, 3 * D, "w_in_bf")
    pvec = cx.sb([128, KC, CW + 3], F32, "pvec")
    hm = cx.sb([128, 2], F32, "hm")
    ones = cx.sb([128, 128], F32, "ones")
    t_c = P.tok("consts")
    P.dma("sync", pvec[:, :, :], pv_d.rearrange("p (k w) -> p k w", k=KC), pwrites=[t_c])
    P.dma("sync", hm[:, :], hm_d, pwrites=[t_c])
    P.op("vector", lambda e: e.memset(ones[:, :], 1.0), pwrites=[t_c])
    P.op("vector", lambda e: e.tensor_copy(out=identf[:, :], in_=R["ident"][:, :]), reads=[R["t_ident"]], pwrites=[t_c])
    A_bc = cx.sb([128, D], F32, "A_bc")
    B_bc = cx.sb([128, D], F32, "B_bc")
    t_AB = P.tok("AB")
    g_bc, t_g = R["tt"].next()
    s_bc, t_s = R["tt"].next()
    P.dma("sync", g_bc[:, :], vecs[0:1, :].broadcast_to([128, D]), writes=[t_g])
    P.dma("sync", s_bc[:, :], vecs[2:3, :].broadcast_to([128, D]), writes=[t_s])
    P.dma("sync", B_bc[:, :], vecs[1:2, :].broadcast_to([128, D]), pwrites=[t_AB])
    P.op("vector", lambda e: e.scalar_tensor_tensor(
        out=A_bc[:, :], in0=s_bc[:, :], scalar=1.0, in1=g_bc[:, :], op0=ALU.add, op1=ALU.mult),
        reads=[t_g, t_s], pwrites=[t_AB])

    hT = cx.sb([128, KC, 640], BF16, "hT")
    t_hT = P.tok("hT")
    Gw = cx.sb([128, KC, NW + 2], BF16, "Gw")
    t_Gw = P.toks("Gw", KC)
    Yb = cx.sb([128, KC, ST], F32, "Yb")
    t_Y = P.toks("Y", KC)
    sig = mk_rot(cx, 4, [128, NW + 2], F32, "sig")
    y2r = mk_rot(cx, 1, [128, ST], F32, "y2")
    stg = mk_rot(cx, 2, [128, ST], F32, "stg")
    lnt = mk_rot(cx, 2, [128, ST], F32, "lnt")
    st1 = cx.ps([128, 512], F32, "st1")
    st2 = cx.ps([128, 512], F32, "st2")
    t_st = P.tok("st")
    mean = cx.sb([128, ST], F32, "mean")
    rstd = cx.sb([128, ST], F32, "rstd")
    nmr = cx.sb([128, ST], F32, "nmr")
    t_ln = P.tok("ln")
    nst = T // ST

    xslot = [0]
    pend = []
    pstat = []

    def flush(final=False):
        while pend:
            pend.pop(0)()
        if final:
            while pstat:
                pstat.pop(0)()

    def mm_pair(wsl, j, acc, n_extra):
        xs = xslot[0] % 2
        xslot[0] += 1
        ax = accx[:, xs * 32:xs * 32 + n_extra]

        def mm(e):
            ins = None
            for kc in range(KC):
                ins = e.matmul(acc[:, 0:ST], lhsT=wsl[:, kc, j * 128:(j + 1) * 128], rhs=hT[:, kc, 0:ST],
                               start=(kc == 0), stop=(kc == KC - 1))
            for kc in range(KC):
                ins = e.matmul(ax, lhsT=wsl[:, kc, j * 128:(j + 1) * 128],
                               rhs=hT[:, kc, ST:ST + n_extra], start=(kc == 0), stop=(kc == KC - 1))
            return ins
        return mm, ax, t_accx[xs]

    for st in range(nst):
        emit_norm_transpose(cx, R, lambda i, st=st: x_d[st * ST + i * 128: st * ST + (i + 1) * 128, :],
                            5, A_bc, B_bc, t_AB, hT, t_hT)
        for q in range(4):
            wg, t_wg = load_slab_bf(cx, R, wb, t_wb, D + q * 512)
            wa, t_wa = load_slab_bf(cx, R, wb, t_wb, q * 512)
            sigs = []
            for j in range(4):
                acc, t_acc = R["acc"].next()
                mmf, ax, t_ax = mm_pair(wg, j, acc, 2 * HALO)
                P.op("tensor", mmf, reads=[t_wg, t_hT], writes=[t_acc, t_ax])
                flush()
                sg, t_sg = sig.next()

                def sg_ev(e, acc=acc, sg=sg, ax=ax):
                    e.activation(out=sg[:, 0:ST], in_=acc[:, 0:ST], func=AF.Sigmoid)
                    return e.activation(out=sg[:, ST:NW], in_=ax, func=AF.Sigmoid)
                P.op("scalar", sg_ev, reads=[t_acc, t_ax], writes=[t_sg])
                sigs.append((sg, t_sg))
            for j in range(4):
                kc = 4 * q + j
                acc, t_acc = R["acc"].next()
                mmf, ax, t_ax = mm_pair(wa, j, acc, 2 * HALO)
                P.op("tensor", mmf, reads=[t_wa, t_hT], writes=[t_acc, t_ax])
                flush()
                sg, t_sg = sigs[j]

                def g_ev(e, acc=acc, sg=sg, kc=kc, st=st, ax=ax):
                    e.tensor_tensor(out=Gw[:, kc, 0:ST], in0=acc[:, 0:ST], in1=sg[:, 0:ST], op=ALU.mult)
                    ins = e.tensor_tensor(out=Gw[:, kc, ST:NW], in0=ax, in1=sg[:, ST:NW], op=ALU.mult)
                    if st == 0:
                        ins = e.tensor_scalar(out=Gw[:, kc, 0:HALO], in0=Gw[:, kc, 0:HALO], scalar1=hm[:, 0:1],
                                              scalar2=None, op0=ALU.mult)
                    if st == nst - 1:
                        ins = e.tensor_scalar(out=Gw[:, kc, NW - HALO:NW], in0=Gw[:, kc, NW - HALO:NW],
                                              scalar1=hm[:, 1:2], scalar2=None, op0=ALU.mult)
                    return ins
                P.op("vector", g_ev, reads=[t_acc, t_ax, t_sg, t_c], writes=[t_Gw[kc]])
                dg, t_dg = dgr.next()

                def mkdiag(e, dg=dg, kc=kc):
                    ins = None
                    for k in range(CW):
                        ins = e.activation(out=dg[:, k, :], in_=identf[:, :], func=AF.Copy, scale=pvec[:, kc, k:k + 1])
                    return ins
                P.op("scalar", mkdiag, reads=[t_c], writes=[t_dg])

                def later(dg=dg, t_dg=t_dg, kc=kc):
                    while pstat:
                        pstat.pop(0)()

                    def dwc(e, dg=dg, kc=kc):
                        ins = None
                        for k in range(CW):
                            ins = e.matmul(yacc[:, :], lhsT=dg[:, k, :], rhs=Gw[:, kc, k:k + ST],
                                           start=(k == 0), stop=(k == CW - 1))
                        return ins
                    P.op("tensor", dwc, reads=[t_dg, t_Gw[kc]], writes=[t_yacc])
                    P.op("vector", lambda e, kc=kc: e.tensor_scalar(
                        out=Yb[:, kc, :], in0=yacc[:, :], scalar1=pvec[:, kc, CW:CW + 1], scalar2=None, op0=ALU.add),
                        reads=[t_yacc, t_c], writes=[t_Y[kc]])
                    y2, t_y2 = y2r.next()
                    P.op("scalar", lambda e, y2=y2, kc=kc: e.activation(out=y2[:, :], in_=Yb[:, kc, :], func=AF.Square),
                         reads=[t_Y[kc]], writes=[t_y2])

                    def stat(e, y2=y2, kc=kc):
                        e.matmul(st1[:, :], lhsT=ones[:, :], rhs=Yb[:, kc, :], start=(kc == 0), stop=(kc == KC - 1))
                        return e.matmul(st2[:, :], lhsT=ones[:, :], rhs=y2[:, :], start=(kc == 0), stop=(kc == KC - 1))
                    pstat.append(lambda kc=kc, stat=stat, t_y2=t_y2: P.op(
                        "tensor", stat, reads=[t_Y[kc], t_y2, t_c] + ([t_ln] if kc == 0 else []),
                        **({"writes": [t_st]} if kc == 0 else {"pwrites": [t_st]})))
                pend.append(later)
        flush(final=True)
        for q in range(4):
            wz, t_wz = load_slab_bf(cx, R, wb, t_wb, 2 * D + q * 512)
            for j in range(4):
                acc, t_acc = R["acc"].next()

                def mmz(e, wz=wz, j=j, acc=acc):
                    ins = None
                    for kc in range(KC):
                        ins = e.matmul(acc[:, 0:ST], lhsT=wz[:, kc, j * 128:(j + 1) * 128],
                                       rhs=hT[:, kc, HALO:HALO + ST], start=(kc == 0), stop=(kc == KC - 1))
                    return ins
                P.op("tensor", mmz, reads=[t_wz, t_hT], writes=[t_acc])
                sb, t_sb = stg.next()
                P.op("scalar", lambda e, acc=acc, sb=sb: e.activation(out=sb[:, :], in_=acc[:, 0:ST], func=AF.Silu),
                     reads=[t_acc], writes=[t_sb])
                c0 = (4 * q + j) * 128
                P.dma("sync", szT_d[c0:c0 + 128, st * ST:(st + 1) * ST], sb[:, :], reads=[t_sb],
                      writes=[P.tok("szT")], semtok=t_sb)

        def ln1(e):
            e.tensor_scalar(out=mean[:, :], in0=st1[:, :], scalar1=1.0 / D, scalar2=None, op0=ALU.mult)
            e.tensor_tensor(out=nmr[:, :], in0=mean[:, :], in1=mean[:, :], op=ALU.mult)
            e.scalar_tensor_tensor(out=rstd[:, :], in0=st2[:, :], scalar=1.0 / D, in1=nmr[:, :],
                                   op0=ALU.mult, op1=ALU.subtract)
            return e.tensor_scalar(out=rstd[:, :], in0=rstd[:, :], scalar1=1e-6, scalar2=None, op0=ALU.add)
        P.op("vector", ln1, reads=[t_st], writes=[t_ln])
        P.op("scalar", lambda e: e.activation(out=rstd[:, :], in_=rstd[:, :], func=AF.Sqrt),
             reads=[t_ln], writes=[t_ln])

        def ln2(e):
            e.reciprocal(out=rstd[:, :], in_=rstd[:, :])
            return e.scalar_tensor_tensor(out=nmr[:, :], in0=mean[:, :], scalar=-1.0, in1=rstd[:, :],
                                          op0=ALU.mult, op1=ALU.mult)
        P.op("vector", ln2, reads=[t_ln], writes=[t_ln])
        for kc in range(KC):
            lt, t_lt = lnt.next()

            def nrm(e, kc=kc, lt=lt):
                e.tensor_tensor(out=lt[:, :], in0=Yb[:, kc, :], in1=rstd[:, :], op=ALU.mult)
                return e.tensor_tensor(out=lt[:, :], in0=lt[:, :], in1=nmr[:, :], op=ALU.add)
            P.op("vector" if kc % 2 == 0 else "gpsimd", nrm, reads=[t_Y[kc], t_ln], writes=[t_lt])
            sb, t_sb = stg.next()
            P.op("scalar", lambda e, kc=kc, lt=lt, sb=sb: e.activation(
                out=sb[:, :], in_=lt[:, :], func=AF.Silu, scale=pvec[:, kc, CW + 1:CW + 2],
                bias=pvec[:, kc, CW + 2:CW + 3]), reads=[t_lt, t_c], writes=[t_sb])
            P.dma("sync", MT_d[kc * 128:(kc + 1) * 128, st * ST:(st + 1) * ST], sb[:, :], reads=[t_sb],
                  writes=[P.tok("MT")], semtok=t_sb)
    P.emit()
    return nc


def run_generic_c(MTs, szTs, x, gate, w_out, final_g=None):
    B, N, _ = x.shape
    TP = N // 4
    ncc = _get_nc("four_c", build_four_c, final=final_g is not None)
    in_maps = []
    for k in range(NCORES):
        b, i = k // 4, k % 4
        vecs = np.stack([gate[b], final_g if final_g is not None else gate[b]]).astype(np.float32)
        in_maps.append({"MT": MTs[k], "szT": szTs[k], "x": np.ascontiguousarray(x[b, i * TP:(i + 1) * TP]),
                        "vecs": vecs, "w_out": w_out})
    rc = _launch(ncc, in_maps)
    return np.stack([np.concatenate([rc[b * 4 + i]["xo"] for i in range(4)], axis=0) for b in range(B)])


def run_conv_layer(x, mod_l, g_l, w_in, dw_w, dw_b, ln_g, ln_b, w_out, tabs, final_g=None):
    B, N, _ = x.shape
    TP = N // 4
    sh, sc, gt = mod_l[:, 0:D], mod_l[:, D:2 * D], mod_l[:, 2 * D:3 * D]
    nca = _get_nc("conv_a", build_conv_a)
    pv = np.concatenate([dw_w.T, dw_b[:, None], ln_g[:, None], ln_b[:, None]], axis=1)
    pv = np.ascontiguousarray(pv.reshape(KC, 128, CW + 3).transpose(1, 0, 2)).reshape(128, KC * (CW + 3))
    in_maps = []
    for k in range(NCORES):
        b, i = k // 4, k % 4
        xe = np.zeros((TP + 2 * HALO + 128, D), np.float32)
        lo, hi = i * TP - HALO, (i + 1) * TP + HALO
        slo, shi = max(lo, 0), min(hi, N)
        xe[slo - lo: shi - lo] = x[b, slo:shi]
        hm = np.zeros((128, 2), np.float32)
        hm[:, 0] = 1.0 if lo >= 0 else 0.0
        hm[:, 1] = 1.0 if hi <= N else 0.0
        vecs = np.stack([g_l, sh[b], sc[b]]).astype(np.float32)
        in_maps.append({"xe": xe, "vecs": vecs, "pvec": pv.astype(np.float32), "hm": hm, "w_in": w_in,
                        "ident": tabs["ident"]})
    ra = _launch(nca, in_maps)
    return run_generic_c([r["MT"] for r in ra], [r["szT"] for r in ra], x, gt, w_out, final_g)


AW = 768


def attn_tables(n_tokens=16384):
    bf = _bf16()
    pos = np.arange(n_tokens)
    row = (pos // 64).astype(np.float32)
    col = (pos % 64).astype(np.float32)
    inv = (10000.0 ** (-np.arange(16, dtype=np.float32) / 16.0)).astype(np.float32)
    ang_r = row[None, :] * inv[:, None]
    ang_c = col[None, :] * inv[:, None]
    ang = np.concatenate([ang_r, ang_r, ang_c, ang_c], axis=0)
    cos = np.cos(ang).astype(np.float32)
    sin = np.sin(ang).astype(np.float32)
    sign = np.concatenate([-np.ones(16), np.ones(16), -np.ones(16), np.ones(16)]).astype(np.float32)
    sinS = sin * sign[:, None]
    t = {"cos": np.concatenate([cos, cos], 0), "sin": np.concatenate([sinS, sinS], 0)}
    R = np.zeros((128, 128), np.float32)
    for hh in range(2):
        for dst in range(64):
            blk = dst // 16
            src = dst + 16 if blk % 2 == 0 else dst - 16
            R[hh * 64 + src, hh * 64 + dst] = 1.0
    t["R"] = R
    kk = np.arange(128)[:, None]
    qq = np.arange(128)[None, :]
    t["maskP"] = (kk >= qq).astype(np.float32).astype(bf)
    t["maskN"] = (kk <= qq).astype(np.float32).astype(bf)
    return t


def build_attn_a(T=4096, NCTX=256):
    nc = bass.Bass("TRN2", target_bir_lowering=False)
    P = Prog(nc)
    cx = Ctx(nc, P)
    TE = T + 256
    x_d = cx.din("xe", [TE, D], F32)
    ctx_d = cx.din("ctx", [NCTX, D], F32)
    vecs = cx.din("vecs", [5, D], F32)
    w_all = cx.din("w_all", [D, 5120], F32)
    ident_d = cx.din("ident", [128, 128], BF16)
    cos_d = cx.din("cos", [128, TE], F32)
    sin_d = cx.din("sin", [128, TE], F32)
    R_d = cx.din("R", [128, 128], F32)
    mk_d = cx.din("masks", [128, 4 * 128], BF16)
    sink_d = cx.din("sink", [128, 32], F32)
    MT_d = cx.dout("MT", [D, T], F32)
    szT_d = cx.dout("szT", [D, T], F32)

    R = std_rots(cx, acc_n=0, with_pT=False)
    R["hb"] = mk_rot(cx, 1, [128, D], BF16, "hb1")
    pp = [cx.ps([128, 1024], F32, f"pp{i}") for i in range(4)]
    t_pp = [P.toks(f"pp{i}_", 2) for i in range(4)]
    R["pT"] = Rot([pp[3][:, 0:512].bitcast(BF16), pp[3][:, 512:1024].bitcast(BF16)], [t_pp[3][0], t_pp[3][1]])
    load_ident(cx, R, ident_d)
    wb, t_wb = precast_weight(cx, w_all, 5120, "w_all_bf")
    Rm = cx.sb([128, 128], F32, "Rm")
    masks = cx.sb([128, 4, 128], BF16, "masks")
    sinkE = cx.sb([128, 32], F32, "sinkE")
    ones = cx.sb([128, 128], BF16, "ones")
    t_c = P.tok("consts")
    P.dma("sync", Rm[:, :], R_d, pwrites=[t_c])
    P.dma("sync", masks[:, :, :], mk_d.rearrange("p (a q) -> p a q", a=4), pwrites=[t_c])
    P.dma("sync", sinkE[:, :], sink_d, pwrites=[t_c])
    P.op("vector", lambda e: e.memset(ones[:, :], 1.0), pwrites=[t_c])
    P.op("scalar", lambda e: e.activation(out=sinkE[:, :], in_=sinkE[:, :], func=AF.Exp), reads=[t_c], writes=[t_c])
    A_bc = cx.sb([128, D], F32, "A_bc")
    B_bc = cx.sb([128, D], F32, "B_bc")
    t_AB = P.tok("AB")

    def set_AB(scale_row, shift_row):
        g_bc, t_g = R["tt"].next()
        s_bc, t_s = R["tt"].next()
        P.dma("sync", g_bc[:, :], vecs[0:1, :].broadcast_to([128, D]), writes=[t_g])
        P.dma("sync", s_bc[:, :], vecs[scale_row:scale_row + 1, :].broadcast_to([128, D]), writes=[t_s])
        P.dma("sync", B_bc[:, :], vecs[shift_row:shift_row + 1, :].broadcast_to([128, D]), pwrites=[t_AB])
        P.op("vector", lambda e: e.scalar_tensor_tensor(
            out=A_bc[:, :], in0=s_bc[:, :], scalar=1.0, in1=g_bc[:, :], op0=ALU.add, op1=ALU.mult),
            reads=[t_g, t_s], pwrites=[t_AB])

    hT = cx.sb([128, KC, AW], BF16, "hT")
    t_hT = P.tok("hT")
    KT = cx.sb([128, 4, AW], BF16, "KT")
    V2 = cx.sb([128, AW // 128, 512], BF16, "V2")
    QT = cx.sb([128, KC, ST], BF16, "QT")
    KTc = cx.sb([128, 4, NCTX], BF16, "KTc")
    Vc2 = cx.sb([128, NCTX // 128, 512], BF16, "Vc2")
    t_KT, t_V2, t_QT, t_KTc, t_Vc2 = P.tok("KT"), P.tok("V2"), P.tok("QT"), P.tok("KTc"), P.tok("Vc2")
    cosb = cx.sb([128, AW], F32, "cosb")
    sinb = cx.sb([128, AW], F32, "sinb")
    t_cs = P.tok("cs")
    qf = mk_rot(cx, 2, [128, AW], F32, "qf")
    r1 = mk_rot(cx, 2, [128, AW], F32, "r1")
    Pj = mk_rot(cx, 6, [128, 512], BF16, "Pj")
    stg = mk_rot(cx, 3, [128, 512], F32, "stg")
    rden = cx.sb([128, 1024], F32, "rden")
    t_rd = P.toks("rden", 2)

    def halves(i):
        return (pp[i][:, 0:512], t_pp[i][0]), (pp[i][:, 512:1024], t_pp[i][1])

    accs = Rot([halves(0)[0][0], halves(0)[1][0], halves(1)[0][0], halves(1)[1][0]],
               [t_pp[0][0], t_pp[0][1], t_pp[1][0], t_pp[1][1]])

    rps = Rot([pp[2][:, 0:512], pp[2][:, 512:1024]], [t_pp[2][0], t_pp[2][1]])
    Ss = Rot([pp[0][:, 0:512], pp[0][:, 512:1024], pp[1][:, 0:512], pp[1][:, 512:1024]],
             [t_pp[0][0], t_pp[0][1], t_pp[1][0], t_pp[1][1]])

    def proj_fm(wsl, t_w, j, col0, ntok):
        acc, t_acc = accs.next()

        def mm(e):
            ins = None
            for kc in range(KC):
                ins = e.matmul(acc[:, 0:ntok], lhsT=wsl[:, kc, j * 128:(j + 1) * 128],
                               rhs=hT[:, kc, col0:col0 + ntok], start=(kc == 0), stop=(kc == KC - 1))
            return ins
        P.op("tensor", mm, reads=[t_w, t_hT], writes=[t_acc])
        return acc, t_acc

    def rope_to(acc, t_acc, ntok, tcol0, dst_ap, dst_tok):
        q, t_q = qf.next()
        P.op("scalar", lambda e: e.copy(out=q[:, 0:ntok], in_=acc[:, 0:ntok]), reads=[t_acc], writes=[t_q])
        rp, t_rp = rps.next()
        P.op("tensor", lambda e: e.matmul(rp[:, 0:ntok], lhsT=Rm[:, :], rhs=q[:, 0:ntok], start=True, stop=True),
             reads=[t_q, t_c], writes=[t_rp])
        r, t_r = r1.next()
        P.op("vector", lambda e: e.tensor_tensor(out=r[:, 0:ntok], in0=rp[:, 0:ntok],
                                                 in1=sinb[:, tcol0:tcol0 + ntok], op=ALU.mult),
             reads=[t_rp, t_cs], writes=[t_r])
        P.op("gpsimd", lambda e: e.tensor_tensor(out=q[:, 0:ntok], in0=q[:, 0:ntok],
                                                 in1=cosb[:, tcol0:tcol0 + ntok], op=ALU.mult),
             reads=[t_cs], writes=[t_q])
        P.op("vector", lambda e: e.tensor_tensor(out=dst_ap, in0=q[:, 0:ntok], in1=r[:, 0:ntok], op=ALU.add),
             reads=[t_q, t_r], pwrites=[dst_tok])

    def v_proj(wsl, t_w, tile_col0, dst_ap, dst_tok):
        acc, t_acc = accs.next()

        def mm(e):
            ins = None
            for kc in range(KC):
                ins = e.matmul(acc[:, :], lhsT=hT[:, kc, tile_col0:tile_col0 + 128], rhs=wsl[:, kc, :],
                               start=(kc == 0), stop=(kc == KC - 1))
            return ins
        P.op("tensor", mm, reads=[t_w, t_hT], writes=[t_acc])
        P.op("scalar", lambda e: e.copy(out=dst_ap, in_=acc[:, :]), reads=[t_acc], pwrites=[dst_tok])

    set_AB(4, 3)
    emit_norm_transpose(cx, R, lambda i: ctx_d[i * 128:(i + 1) * 128, :], NCTX // 128, A_bc, B_bc, t_AB, hT, t_hT)
    wk, t_wk = load_slab_bf(cx, R, wb, t_wb, 2048)
    for j in range(4):
        acc, t_acc = proj_fm(wk, t_wk, j, 0, NCTX)
        P.op("vector", lambda e, acc=acc, j=j: e.tensor_copy(out=KTc[:, j, :], in_=acc[:, 0:NCTX]),
             reads=[t_acc], pwrites=[t_KTc])
    wv, t_wv = load_slab_bf(cx, R, wb, t_wb, 2560)
    for i in range(NCTX // 128):
        v_proj(wv, t_wv, i * 128, Vc2[:, i, :], t_Vc2)

    set_AB(2, 1)
    nst = T // ST
    for st in range(nst):
        w0 = st * ST
        P.dma("sync", cosb[:, :], cos_d[:, w0:w0 + AW], pwrites=[t_cs])
        P.dma("sync", sinb[:, :], sin_d[:, w0:w0 + AW], pwrites=[t_cs])
        emit_norm_transpose(cx, R, lambda i, w0=w0: x_d[w0 + i * 128: w0 + (i + 1) * 128, :],
                            AW // 128, A_bc, B_bc, t_AB, hT, t_hT)
        wk, t_wk = load_slab_bf(cx, R, wb, t_wb, 2048)
        for j in range(4):
            for (c0, n) in ((0, 512), (512, AW - 512)):
                acc, t_acc = proj_fm(wk, t_wk, j, c0, n)
                rope_to(acc, t_acc, n, c0, KT[:, j, c0:c0 + n], t_KT)
        wv, t_wv = load_slab_bf(cx, R, wb, t_wb, 2560)
        for i in range(AW // 128):
            v_proj(wv, t_wv, i * 128, V2[:, i, :], t_V2)
        for q in range(4):
            wq, t_wq = load_slab_bf(cx, R, wb, t_wb, q * 512)
            for j in range(4):
                acc, t_acc = proj_fm(wq, t_wq, j, 128, ST)
                rope_to(acc, t_acc, ST, 128, QT[:, 4 * q + j, :], t_QT)
        for q in range(4):
            wz, t_wz = load_slab_bf(cx, R, wb, t_wb, 3072 + q * 512)
            for j in range(4):
                acc, t_acc = proj_fm(wz, t_wz, j, 128, ST)
                sb, t_sb = stg.next()
                P.op("scalar", lambda e, acc=acc, sb=sb: e.activation(out=sb[:, :], in_=acc[:, :], func=AF.Silu),
                     reads=[t_acc], writes=[t_sb])
                c0 = (4 * q + j) * 128
                P.dma("sync", szT_d[c0:c0 + 128, st * ST:(st + 1) * ST], sb[:, :], reads=[t_sb],
                      writes=[P.tok("szT")], semtok=t_sb)
        units = [(b, h, j, par) for b in range(ST // 128) for h in range(4) for par in range(2) for j in range(5)]
        LOOK = 3
        sbufs = {}

        def emit_S(u):
            b, h, j, par = units[u]
            Sb, t_S = Ss.next()
            if j < 3:
                kt = KT[par * 64:(par + 1) * 64, h, (b + j) * 128:(b + j + 1) * 128]
                rk = t_KT
            else:
                kt = KTc[par * 64:(par + 1) * 64, h, (j - 3) * 128:(j - 2) * 128]
                rk = t_KTc
            P.op("tensor", lambda e: e.matmul(Sb[:, :], lhsT=kt,
                                              rhs=QT[par * 64:(par + 1) * 64, 4 * h:4 * h + 4, b * 128:(b + 1) * 128],
                                              start=True, stop=True), reads=[rk, t_QT], writes=[t_S])
            sbufs[u] = (Sb, t_S)

        for u in range(min(LOOK, len(units))):
            emit_S(u)
        stage = {}
        for u, (b, h, j, par) in enumerate(units):
            if u + LOOK < len(units):
                emit_S(u + LOOK)
            Sb, t_S = sbufs.pop(u)
            pj, t_pj = Pj.next()
            P.op("scalar", lambda e, Sb=Sb, pj=pj: e.activation(out=pj[:, 0:512], in_=Sb[:, :], func=AF.Exp, scale=0.125),
                 reads=[t_S], writes=[t_pj])
            mi = None
            if j == 0:
                mi = 2 if (st == 0 and b == 0) else 0
            elif j == 2:
                mi = 3 if (st == nst - 1 and b == ST // 128 - 1) else 1
            if mi is not None:
                P.op("gpsimd", lambda e, pj=pj, mi=mi: e.tensor_tensor(
                    out=pj[:, 0:512].rearrange("p (g q) -> p g q", g=4),
                    in0=pj[:, 0:512].rearrange("p (g q) -> p g q", g=4),
                    in1=masks[:, mi, :].unsqueeze(1).broadcast_to([128, 4, 128]), op=ALU.mult),
                    reads=[t_c], writes=[t_pj])
            if j < 3:
                v_ap, rv = V2[:, b + j, h * 128:(h + 1) * 128], t_V2
            else:
                v_ap, rv = Vc2[:, j - 3, h * 128:(h + 1) * 128], t_Vc2
            slot = ((b * 4 + h) * 2 + par) % 2
            sl = slice(slot * 512, (slot + 1) * 512)

            def mm_o(e, pj=pj, v_ap=v_ap, j=j, sl=sl):
                e.matmul(pp[2][:, sl], lhsT=v_ap, rhs=pj[:, 0:512], start=(j == 0), stop=(j == 4))
                return e.matmul(pp[3][:, sl], lhsT=ones[:, :], rhs=pj[:, 0:512], start=(j == 0), stop=(j == 4))
            kw = {"writes": [t_pp[2][slot], t_pp[3][slot]]} if j == 0 else {"pwrites": [t_pp[2][slot], t_pp[3][slot]]}
            P.op("tensor", mm_o, reads=[t_pj, rv, t_c], **kw)
            if j == 4:
                ps_ = slice(par * 64, (par + 1) * 64)
                if par == 0:
                    stage[(b, h)] = stg.next()
                sb, t_sb = stage[(b, h)]

                def fin(e, h=h, par=par, sl=sl, ps_=ps_, sb=sb):
                    e.tensor_tensor(out=rden[ps_, sl].rearrange("p (g q) -> p g q", g=4),
                                    in0=pp[3][ps_, sl].rearrange("p (g q) -> p g q", g=4),
                                    in1=sinkE[ps_, h * 8 + par * 4:h * 8 + par * 4 + 4].unsqueeze(2).broadcast_to([64, 4, 128]),
                                    op=ALU.add)
                    e.reciprocal(out=rden[ps_, sl], in_=rden[ps_, sl])
                    return e.tensor_tensor(out=sb[ps_, :], in0=pp[2][ps_, sl], in1=rden[ps_, sl], op=ALU.mult)
                if par == 0:
                    P.op("vector", fin, reads=[t_pp[2][slot], t_pp[3][slot], t_c], writes=[t_rd[slot], t_sb])
                else:
                    P.op("vector", fin, reads=[t_pp[2][slot], t_pp[3][slot], t_c], writes=[t_rd[slot]], pwrites=[t_sb])
                if par == 1:
                    q0 = st * ST + b * 128
                    P.dma("sync", MT_d[h * 512:(h + 1) * 512, q0:q0 + 128].rearrange("(m p) q -> p m q", p=128),
                          sb[:, :].rearrange("p (m q) -> p m q", m=4), reads=[t_sb], writes=[P.tok("MT")], semtok=t_sb)
    P.emit()
    return nc


def run_attn_layer(x, ctx, mod_l, g_l, w_in, sink, w_out, tabs, atabs, final_g=None):
    B, N, _ = x.shape
    TP = N // 4
    bf = _bf16()
    sh, sc, gt = mod_l[:, 0:D], mod_l[:, D:2 * D], mod_l[:, 2 * D:3 * D]
    nca = _get_nc("attn_a", build_attn_a)
    wq, wk, wv, wz = w_in[:, 0:2048], w_in[:, 2048:2304], w_in[:, 2304:2560], w_in[:, 2560:4608]
    wkd = np.repeat(wk.reshape(D, 4, 1, 64), 2, axis=2).reshape(D, 512)
    wvd = np.repeat(wv.reshape(D, 4, 1, 64), 2, axis=2).reshape(D, 512)
    w_all = np.ascontiguousarray(np.concatenate([wq, wkd, wvd, wz], axis=1))
    sp = sink.reshape(4, 4, 2).transpose(0, 2, 1).reshape(32)
    sink_b = np.ascontiguousarray(np.broadcast_to(sp[None, :], (128, 32))).astype(np.float32)
    in_maps = []
    for k in range(NCORES):
        b, i = k // 4, k % 4
        lo, hi = i * TP - 128, (i + 1) * TP + 128
        xe = np.zeros((TP + 256, D), np.float32)
        cosw = np.zeros((128, TP + 256), np.float32)
        sinw = np.zeros((128, TP + 256), np.float32)
        slo, shi = max(lo, 0), min(hi, N)
        xe[slo - lo: shi - lo] = x[b, slo:shi]
        cosw[:, slo - lo: shi - lo] = atabs["cos"][:, slo:shi]
        sinw[:, slo - lo: shi - lo] = atabs["sin"][:, slo:shi]
        zero = np.zeros((128, 128), bf)
        mk = np.concatenate([atabs["maskP"], atabs["maskN"],
                             atabs["maskP"] if lo >= 0 else zero,
                             atabs["maskN"] if hi <= N else zero], axis=1)
        vecs = np.stack([g_l, sh[b], sc[b], sh[2], sc[2]]).astype(np.float32)
        in_maps.append({"xe": xe, "ctx": np.ascontiguousarray(ctx[b]), "vecs": vecs, "w_all": w_all,
                        "ident": tabs["ident"], "cos": cosw, "sin": sinw, "R": atabs["R"],
                        "masks": np.ascontiguousarray(mk), "sink": sink_b})
    ra = _launch(nca, in_maps)
    return run_generic_c([r["MT"] for r in ra], [r["szT"] for r in ra], x, gt, w_out, final_g)


def kernel(x, c, ctx, c_ctx, norm_g, ada_w, ada_b, four_w_in, four_w_out, attn_w_in, attn_sink, attn_w_out,
           conv_w_in, conv_dw_w, conv_dw_b, conv_ln_g, conv_ln_b, conv_w_out, final_g):
    f = lambda a: np.ascontiguousarray(np.asarray(a, dtype=np.float32))
    x, c, ctx, c_ctx, norm_g, ada_w, ada_b = map(f, (x, c, ctx, c_ctx, norm_g, ada_w, ada_b))
    four_w_in, four_w_out, attn_w_in, attn_sink, attn_w_out = map(f, (four_w_in, four_w_out, attn_w_in, attn_sink, attn_w_out))
    conv_w_in, conv_dw_w, conv_dw_b, conv_ln_g, conv_ln_b, conv_w_out, final_g = map(
        f, (conv_w_in, conv_dw_w, conv_dw_b, conv_ln_g, conv_ln_b, conv_w_out, final_g))
    tabs = const_tables()
    atabs = attn_tables(x.shape[1])
    mod = run_mod(c, c_ctx, ada_w, ada_b)
    x1, ctx1 = run_fourier_layer(x, ctx, mod[0], norm_g[0], four_w_in[0], four_w_out[0], tabs)
    x2 = run_attn_layer(x1, ctx1, mod[1], norm_g[1], attn_w_in[0], attn_sink[0], attn_w_out[0], tabs, atabs)
    x3 = run_conv_layer(x2, mod[2], norm_g[2], conv_w_in[0], conv_dw_w[0], conv_dw_b[0], conv_ln_g[0],
                        conv_ln_b[0], conv_w_out[0], tabs)
    out, _ = run_fourier_layer(x3, ctx1, mod[3], norm_g[3], four_w_in[1], four_w_out[1], tabs, final_g=final_g)
    return out.astype(np.float32)
```
